# Optimizing a Trainium2 kernel written in Bass

```python
import math
import jax
import jax.numpy as jnp
from jax import lax
import numpy as np

D_MODEL = 2048
BATCH = 16
SEQ = 2048
DEPTH = 4

CTX_LEN = 256
GRID_W = 64
N_MOD = 9
D_FF = 5632
S5_WIDTH = D_MODEL // 4
S5_GROUP = 16
S5_GROUPS = S5_WIDTH // S5_GROUP
S5_STATE = 64
S5_DT_MIN = 0.001
S5_DT_MAX = 0.1
RWKV_WIDTH = D_MODEL - S5_WIDTH
RWKV_HEAD = 64
RWKV_HEADS = RWKV_WIDTH // RWKV_HEAD
DECAY_LORA = 64
ICLR_LORA = 64
VRES_LORA = 64
GATE_LORA = 224
IN_COLS = S5_WIDTH + 4 * RWKV_WIDTH
NORM_EPS = 1e-6
GN_EPS = 64e-5
L2_EPS = 1e-12

kernel_name = 'hybrid_s5_rwkv7_prefix_dit'


def rms_norm(x, gain):
    xf = x.astype(jnp.float32)
    y = xf * lax.rsqrt(jnp.mean(xf * xf, axis=-1, keepdims=True) + NORM_EPS)
    return (y * gain.astype(jnp.float32)).astype(x.dtype)


def modulate(h, shift, scale):
    return h * (1.0 + scale) + shift


def swiglu(h, w1, w2):
    gate, up = jnp.split(h @ w1, 2, axis=-1)
    return (jax.nn.silu(gate) * up) @ w2


def ffn_half_step(x, gain, shift, scale, gate, w1, w2):
    h = modulate(rms_norm(x, gain), shift, scale)
    return x + 0.5 * gate * swiglu(h, w1, w2)


def s5_discretize(lam_re, lam_im, log_dt, b_re, b_im):
    lam_re = lam_re.astype(jnp.float32)
    lam_im = lam_im.astype(jnp.float32)
    b_re = b_re.astype(jnp.float32)
    b_im = b_im.astype(jnp.float32)
    dt = jnp.exp(log_dt.astype(jnp.float32))[:, None]
    mag = jnp.exp(lam_re * dt)
    abar_re = mag * jnp.cos(lam_im * dt)
    abar_im = mag * jnp.sin(lam_im * dt)
    den = lam_re * lam_re + lam_im * lam_im
    num_re = abar_re - 1.0
    f_re = (num_re * lam_re + abar_im * lam_im) / den
    f_im = (abar_im * lam_re - num_re * lam_im) / den
    bb_re = f_re[..., None] * b_re - f_im[..., None] * b_im
    bb_im = f_re[..., None] * b_im + f_im[..., None] * b_re
    return abar_re, abar_im, bb_re, bb_im


def complex_affine_combine(earlier, later):
    a1r, a1i, b1r, b1i = earlier
    a2r, a2i, b2r, b2i = later
    return (a2r * a1r - a2i * a1i,
            a2r * a1i + a2i * a1r,
            a2r * b1r - a2i * b1i + b2r,
            a2r * b1i + a2i * b1r + b2i)


def s5_scan(u, abar_re, abar_im, bb_re, bb_im, h0, reverse):
    bu_re = jnp.einsum('btgh,gph->btgp', u, bb_re)
    bu_im = jnp.einsum('btgh,gph->btgp', u, bb_im)
    if h0 is not None:
        h0_re, h0_im = h0
        first = -1 if reverse else 0
        bu_re = bu_re.at[:, first].add(abar_re * h0_re - abar_im * h0_im)
        bu_im = bu_im.at[:, first].add(abar_re * h0_im + abar_im * h0_re)
    t = u.shape[1]
    a_re = jnp.broadcast_to(abar_re, (1, t) + abar_re.shape)
    a_im = jnp.broadcast_to(abar_im, (1, t) + abar_im.shape)
    _, _, s_re, s_im = lax.associative_scan(
        complex_affine_combine, (a_re, a_im, bu_re, bu_im), reverse=reverse, axis=1)
    return s_re, s_im


def s5_readout(s_re, s_im, c_re, c_im):
    return (jnp.einsum('btgp,ghp->btgh', s_re, c_re)
            - jnp.einsum('btgp,ghp->btgh', s_im, c_im))


def s5_glu(y, w, bias):
    y = jax.nn.gelu(y)
    return y * jax.nn.sigmoid(y @ w + bias)


def s5_mixer(u_lat, u_ctx, lam_re, lam_im, log_dt, b_re, b_im, c_re, c_im, d,
             glu_w, glu_b, with_ctx_out):
    b, t, _ = u_lat.shape
    tc = u_ctx.shape[1]
    ul = u_lat.reshape(b, t, S5_GROUPS, S5_GROUP)
    uc = u_ctx.reshape(b, tc, S5_GROUPS, S5_GROUP)
    y_lat = d * u_lat
    y_ctx = d * u_ctx if with_ctx_out else None
    for direction, reverse in enumerate((False, True)):
        abar_re, abar_im, bb_re, bb_im = s5_discretize(
            lam_re[direction], lam_im[direction], log_dt[direction],
            b_re[direction], b_im[direction])
        cr = c_re[direction].astype(jnp.float32)
        ci = c_im[direction].astype(jnp.float32)
        sc_re, sc_im = s5_scan(uc, abar_re, abar_im, bb_re, bb_im, None, reverse)
        edge = 0 if reverse else -1
        sl_re, sl_im = s5_scan(ul, abar_re, abar_im, bb_re, bb_im,
                               (sc_re[:, edge], sc_im[:, edge]), reverse)
        y_lat = y_lat + s5_readout(sl_re, sl_im, cr, ci).reshape(b, t, S5_WIDTH)
        if with_ctx_out:
            y_ctx = y_ctx + s5_readout(sc_re, sc_im, cr, ci).reshape(b, tc, S5_WIDTH)
    out_lat = s5_glu(y_lat, glu_w, glu_b)
    out_ctx = s5_glu(y_ctx, glu_w, glu_b) if with_ctx_out else None
    return out_lat, out_ctx


def grid_qshift(f):
    b, l, s, ch = f.shape
    rows = l // GRID_W
    q = ch // 4
    g = f.reshape(b, rows, GRID_W, s, 4, q)
    zc = jnp.zeros((b, rows, 1, s, q), f.dtype)
    zr = jnp.zeros((b, 1, GRID_W, s, q), f.dtype)
    from_left = jnp.concatenate([zc, g[:, :, :-1, :, 0]], axis=2)
    from_right = jnp.concatenate([g[:, :, 1:, :, 1], zc], axis=2)
    from_above = jnp.concatenate([zr, g[:, :-1, :, :, 2]], axis=1)
    from_below = jnp.concatenate([g[:, 1:, :, :, 3], zr], axis=1)
    return jnp.stack([from_left, from_right, from_above, from_below], axis=4).reshape(b, l, s, ch)


def seq_shift(f):
    b, t, s, ch = f.shape
    h = f.reshape(b, t, s, 2, ch // 2)
    z = jnp.zeros((b, 1, s, ch // 2), f.dtype)
    prev = jnp.concatenate([z, h[:, :-1, :, 0]], axis=1)
    nxt = jnp.concatenate([h[:, 1:, :, 1], z], axis=1)
    return jnp.stack([prev, nxt], axis=3).reshape(b, t, s, ch)


def l2_normalize_heads(x):
    b, t, _ = x.shape
    xh = x.reshape(b, t, RWKV_HEADS, RWKV_HEAD)
    n = jnp.sqrt(jnp.sum(xh * xh, axis=-1, keepdims=True))
    return (xh / jnp.maximum(n, L2_EPS)).reshape(b, t, RWKV_WIDTH)


def rwkv_state_inputs(f, f_shift, mu, w0, w1, w2, a0, a1, a2, k_k, k_a, vres, v_first):
    k = f[:, :, 1] + mu[1] * (f_shift[:, :, 1] - f[:, :, 1])
    v = f[:, :, 2] + mu[2] * (f_shift[:, :, 2] - f[:, :, 2])
    z = f[:, :, 3]
    dz = f_shift[:, :, 3] - z
    xw = z + mu[3] * dz
    xa = z + mu[4] * dz
    if vres is None:
        v_first = v
    else:
        mu_m, v0, v1, v2 = vres
        xm = z + mu_m * dz
        v = v + (v_first - v) * jax.nn.sigmoid(v0 + (xm @ v1) @ v2)
    kk = l2_normalize_heads(k * k_k)
    dirs = []
    for direction in range(2):
        w_log = -jax.nn.softplus(-(w0[direction] + jnp.tanh(xw @ w1[direction]) @ w2[direction])) - 0.5
        iclr = jax.nn.sigmoid(a0[direction] + (xa @ a1[direction]) @ a2[direction])
        decay = jnp.exp(-jnp.exp(w_log))
        k_dir = k * (1.0 + (iclr - 1.0) * k_a)
        dirs.append((decay, k_dir, kk * iclr))
    return v, kk, dirs, v_first


def rwkv_readout_inputs(f, f_shift, mu, g1, g2):
    r = f[:, :, 0] + mu[0] * (f_shift[:, :, 0] - f[:, :, 0])
    xg = f[:, :, 3] + mu[5] * (f_shift[:, :, 3] - f[:, :, 3])
    return r, jax.nn.sigmoid(xg @ g1) @ g2


def rwkv_scan(r, w, k, v, a, bvec, s0, reverse):
    b, t, _ = w.shape
    emit = r is not None

    def to_heads(z):
        return jnp.swapaxes(z.reshape(b, t, RWKV_HEADS, RWKV_HEAD), 0, 1)

    def step(state, inp):
        w_t, k_t, v_t, a_t, b_t = inp[:5]
        sa = jnp.einsum('bhij,bhj->bhi', state, a_t)
        state = (state * w_t[:, :, None, :] + sa[..., None] * b_t[:, :, None, :]
                 + v_t[..., None] * k_t[:, :, None, :])
        y = jnp.einsum('bhij,bhj->bhi', state, inp[5]) if emit else None
        return state, y

    streams = (w, k, v, a, bvec) + ((r,) if emit else ())
    s_final, ys = lax.scan(step, s0, tuple(to_heads(z) for z in streams), reverse=reverse)
    y = jnp.swapaxes(ys, 0, 1).reshape(b, t, RWKV_WIDTH) if emit else None
    return s_final, y


def rwkv_output(y_sum, r, v, k_fwd, k_bwd, g, r_k, ln_w, ln_b):
    b, t, _ = y_sum.shape
    hs = (b, t, RWKV_HEADS, RWKV_HEAD)
    yh = y_sum.reshape(hs)
    mean = jnp.mean(yh, axis=-1, keepdims=True)
    var = jnp.mean(jnp.square(yh - mean), axis=-1, keepdims=True)
    yn = ((yh - mean) * lax.rsqrt(var + GN_EPS)).reshape(b, t, RWKV_WIDTH) * ln_w + ln_b
    rh = r.reshape(hs)
    score = (jnp.sum(rh * k_fwd.reshape(hs) * r_k, axis=-1, keepdims=True)
             + jnp.sum(rh * k_bwd.reshape(hs) * r_k, axis=-1, keepdims=True))
    bonus = (score * v.reshape(hs)).reshape(b, t, RWKV_WIDTH)
    return (yn + bonus) * g


def rwkv_mixer(f_lat, f_ctx, mu, w0, w1, w2, a0, a1, a2, g1, g2, k_k, k_a, r_k, ln_w, ln_b,
               vres, v_first_lat, v_first_ctx, with_ctx_out):
    sh_lat = grid_qshift(f_lat)
    sh_ctx = seq_shift(f_ctx)
    v_l, kk_l, dirs_l, v_first_lat = rwkv_state_inputs(
        f_lat, sh_lat, mu, w0, w1, w2, a0, a1, a2, k_k, k_a, vres, v_first_lat)
    v_c, kk_c, dirs_c, v_first_ctx = rwkv_state_inputs(
        f_ctx, sh_ctx, mu, w0, w1, w2, a0, a1, a2, k_k, k_a, vres, v_first_ctx)
    r_l, g_l = rwkv_readout_inputs(f_lat, sh_lat, mu, g1, g2)
    if with_ctx_out:
        r_c, g_c = rwkv_readout_inputs(f_ctx, sh_ctx, mu, g1, g2)
    else:
        r_c, g_c = None, None
    s0 = jnp.zeros((f_lat.shape[0], RWKV_HEADS, RWKV_HEAD, RWKV_HEAD), jnp.float32)
    ys_l, ys_c = [], []
    for direction, reverse in enumerate((False, True)):
        dec_c, kd_c, b_c = dirs_c[direction]
        s_ctx, y_c = rwkv_scan(r_c, dec_c, kd_c, v_c, -kk_c, b_c, s0, reverse)
        dec_l, kd_l, b_l = dirs_l[direction]
        _, y_l = rwkv_scan(r_l, dec_l, kd_l, v_l, -kk_l, b_l, s_ctx, reverse)
        ys_l.append(y_l)
        ys_c.append(y_c)
    out_lat = rwkv_output(ys_l[0] + ys_l[1], r_l, v_l, dirs_l[0][1], dirs_l[1][1], g_l,
                          r_k, ln_w, ln_b)
    if with_ctx_out:
        out_ctx = rwkv_output(ys_c[0] + ys_c[1], r_c, v_c, dirs_c[0][1], dirs_c[1][1], g_c,
                              r_k, ln_w, ln_b)
    else:
        out_ctx = None
    return out_lat, out_ctx, v_first_lat, v_first_ctx


def setup_inputs(seed: int = 0) -> dict:
    key = jax.random.key(seed)
    keys = iter(jax.random.split(key, 64))

    def normal(shape, scale):
        return scale * jax.random.normal(next(keys), shape, jnp.float32)

    def uniform(shape, lo, hi):
        return jax.random.uniform(next(keys), shape, jnp.float32, lo, hi)

    L, D, C = DEPTH, D_MODEL, RWKV_WIDTH
    G, P, HG = S5_GROUPS, S5_STATE, S5_GROUP
    n_state = jnp.arange(P, dtype=jnp.float32)
    ratio = jnp.arange(C, dtype=jnp.float32) / (C - 1)
    decay_base = -7.0 + 5.0 * ratio ** 0.85 + 0.5
    return {
        'x': normal((BATCH, SEQ, D), 1.0),
        'c': normal((BATCH, D), 1.0),
        'ctx': normal((BATCH, CTX_LEN, D), 1.0),
        'c_ctx': normal((D,), 1.0),
        'ada_w': normal((L, D, N_MOD * D), 0.5 * D ** -0.5),
        'ada_b': normal((L, N_MOD * D), 0.02),
        'norm_g': 1.0 + normal((L, 3, D), 0.02),
        'ffn_w1': normal((L, 2, D, 2 * D_FF), D ** -0.5),
        'ffn_w2': normal((L, 2, D_FF, D), D_FF ** -0.5),
        'w_in': normal((L, D, IN_COLS), D ** -0.5),
        'w_out': normal((L, D, D), D ** -0.5),
        's5_lam_re': -0.5 + normal((L, 2, G, P), 0.01),
        's5_lam_im': math.pi * n_state + normal((L, 2, G, P), 0.01),
        's5_log_dt': uniform((L, 2, G), math.log(S5_DT_MIN), math.log(S5_DT_MAX)),
        's5_b_re': normal((L, 2, G, P, HG), (2 * HG) ** -0.5),
        's5_b_im': normal((L, 2, G, P, HG), (2 * HG) ** -0.5),
        's5_c_re': normal((L, 2, G, HG, P), P ** -0.5),
        's5_c_im': normal((L, 2, G, HG, P), P ** -0.5),
        's5_d': normal((L, S5_WIDTH), 1.0),
        's5_glu_w': normal((L, S5_WIDTH, S5_WIDTH), S5_WIDTH ** -0.5),
        's5_glu_b': normal((L, S5_WIDTH), 0.02),
        'rwkv_mu': uniform((L, 6, C), 0.0, 1.0),
        'rwkv_w0': decay_base + normal((L, 2, C), 0.1),
        'rwkv_w1': normal((L, 2, C, DECAY_LORA), C ** -0.5),
        'rwkv_w2': normal((L, 2, DECAY_LORA, C), 0.1 * DECAY_LORA ** -0.5),
        'rwkv_a0': normal((L, 2, C), 0.1),
        'rwkv_a1': normal((L, 2, C, ICLR_LORA), C ** -0.5),
        'rwkv_a2': normal((L, 2, ICLR_LORA, C), 0.5 * ICLR_LORA ** -0.5),
        'rwkv_g1': normal((L, C, GATE_LORA), C ** -0.5),
        'rwkv_g2': normal((L, GATE_LORA, C), GATE_LORA ** -0.5),
        'rwkv_k_k': 0.85 + normal((L, C), 0.02),
        'rwkv_k_a': 1.0 + normal((L, C), 0.02),
        'rwkv_r_k': normal((L, RWKV_HEADS, RWKV_HEAD), 0.1),
        'rwkv_ln_w': 1.0 + normal((L, C), 0.02),
        'rwkv_ln_b': normal((L, C), 0.02),
        'rwkv_mu_m': uniform((L - 1, C), 0.0, 1.0),
        'rwkv_v0': 1.0 + normal((L - 1, C), 0.1),
        'rwkv_v1': normal((L - 1, C, VRES_LORA), C ** -0.5),
        'rwkv_v2': normal((L - 1, VRES_LORA, C), 0.5 * VRES_LORA ** -0.5),
        'final_g': 1.0 + normal((D,), 0.02),
    }


def reference(x, c, ctx, c_ctx, ada_w, ada_b, norm_g, ffn_w1, ffn_w2, w_in, w_out,
              s5_lam_re, s5_lam_im, s5_log_dt, s5_b_re, s5_b_im, s5_c_re, s5_c_im, s5_d,
              s5_glu_w, s5_glu_b, rwkv_mu, rwkv_w0, rwkv_w1, rwkv_w2, rwkv_a0, rwkv_a1, rwkv_a2,
              rwkv_g1, rwkv_g2, rwkv_k_k, rwkv_k_a, rwkv_r_k, rwkv_ln_w, rwkv_ln_b,
              rwkv_mu_m, rwkv_v0, rwkv_v1, rwkv_v2, final_g):
    b, t, _ = x.shape
    tc = ctx.shape[1]
    silu_c = jax.nn.silu(c)[:, None, :]
    silu_cc = jax.nn.silu(c_ctx)
    xl, xc = x, ctx
    v_first_lat, v_first_ctx = None, None
    for layer in range(DEPTH):
        with_ctx_out = layer < DEPTH - 1
        ml = jnp.split(silu_c @ ada_w[layer] + ada_b[layer], N_MOD, axis=-1)
        mc = jnp.split(silu_cc @ ada_w[layer] + ada_b[layer], N_MOD, axis=-1)
        gn = norm_g[layer]

        xl = ffn_half_step(xl, gn[0], ml[0], ml[1], ml[2], ffn_w1[layer, 0], ffn_w2[layer, 0])
        xc = ffn_half_step(xc, gn[0], mc[0], mc[1], mc[2], ffn_w1[layer, 0], ffn_w2[layer, 0])

        pl = (modulate(rms_norm(xl, gn[1]), ml[3], ml[4]) @ w_in[layer]).astype(jnp.float32)
        pc = (modulate(rms_norm(xc, gn[1]), mc[3], mc[4]) @ w_in[layer]).astype(jnp.float32)
        s5_l, s5_c = s5_mixer(
            pl[..., :S5_WIDTH], pc[..., :S5_WIDTH],
            s5_lam_re[layer], s5_lam_im[layer], s5_log_dt[layer], s5_b_re[layer], s5_b_im[layer],
            s5_c_re[layer], s5_c_im[layer], s5_d[layer], s5_glu_w[layer], s5_glu_b[layer],
            with_ctx_out)
        f_lat = pl[..., S5_WIDTH:].reshape(b, t, 4, RWKV_WIDTH)
        f_ctx = pc[..., S5_WIDTH:].reshape(b, tc, 4, RWKV_WIDTH)
        if layer == 0:
            vres = None
        else:
            vres = (rwkv_mu_m[layer - 1], rwkv_v0[layer - 1], rwkv_v1[layer - 1], rwkv_v2[layer - 1])
        rw_l, rw_c, v_first_lat, v_first_ctx = rwkv_mixer(
            f_lat, f_ctx, rwkv_mu[layer], rwkv_w0[layer], rwkv_w1[layer], rwkv_w2[layer],
            rwkv_a0[layer], rwkv_a1[layer], rwkv_a2[layer], rwkv_g1[layer], rwkv_g2[layer],
            rwkv_k_k[layer], rwkv_k_a[layer], rwkv_r_k[layer], rwkv_ln_w[layer], rwkv_ln_b[layer],
            vres, v_first_lat, v_first_ctx, with_ctx_out)
        mix_l = jnp.concatenate([s5_l, rw_l], axis=-1).astype(xl.dtype) @ w_out[layer]
        xl = xl + ml[5] * mix_l

        xl = ffn_half_step(xl, gn[2], ml[6], ml[7], ml[8], ffn_w1[layer, 1], ffn_w2[layer, 1])
        if with_ctx_out:
            mix_c = jnp.concatenate([s5_c, rw_c], axis=-1).astype(xc.dtype) @ w_out[layer]
            xc = xc + mc[5] * mix_c
            xc = ffn_half_step(xc, gn[2], mc[6], mc[7], mc[8], ffn_w1[layer, 1], ffn_w2[layer, 1])
    return rms_norm(xl, final_g)
```

```python
import math
from contextlib import ExitStack
import numpy as np
import concourse.bass as bass
import concourse.mybir as mybir
from concourse.bass_utils import run_bass_kernel_spmd

F32 = mybir.dt.float32
BF16 = mybir.dt.bfloat16
ALU = mybir.AluOpType
AF = mybir.ActivationFunctionType
AX = mybir.AxisListType

D = 2048
DTL = 16
S5W = 512
RW = 1536
RT = 12
NH = 24
INC = S5W + 4 * RW
NMOD = 9
NORM_EPS = 1e-6
GN_EPS = 64e-5
GRID = 64

_UC = [0]


def _u(name):
    _UC[0] += 1
    return "%s_u%d" % (name, _UC[0])


FULL_CFG = dict(NB=2, T=2048, TC=256, DEPTH=4, DFF=5632)


class Buf:
    __slots__ = ("name", "w", "r")

    def __init__(self, name=""):
        self.name = name
        self.w = None
        self.r = []


class K:
    def __init__(self, nc, n_dma_sems=32):
        self.nc = nc
        self.eng = {"pe": nc.tensor, "dve": nc.vector, "act": nc.scalar,
                    "pool": nc.gpsimd, "sp": nc.sync}
        self.sems = []
        self.cnt = []
        self.esem = {}
        for e in self.eng:
            self.esem[e] = self._new_sem("s_" + e)
        self.dsem = [self._new_sem("d%d" % i) for i in range(n_dma_sems)]
        self.dnext = 0
        self.known = {e: {} for e in self.eng}
        self.ninst = 0

    def _new_sem(self, name):
        s = self.nc.alloc_semaphore(name)
        self.sems.append(s)
        self.cnt.append(0)
        return len(self.sems) - 1

    def _wait(self, e, si, val):
        kn = self.known[e]
        if kn.get(si, 0) >= val:
            return
        self.eng[e].wait_ge(self.sems[si], val)
        kn[si] = val
        self.ninst += 1

    def _deps(self, e, reads, writes):
        own = self.esem[e]
        need = {}
        for b in reads:
            if b.w is not None:
                si, v = b.w
                if si == own and e == "pe":
                    continue
                if need.get(si, 0) < v:
                    need[si] = v
        for b in writes:
            if b.w is not None:
                si, v = b.w
                if not (si == own and e == "pe"):
                    if need.get(si, 0) < v:
                        need[si] = v
            for (si, v) in b.r:
                if si == own:
                    continue
                if need.get(si, 0) < v:
                    need[si] = v
        for si, v in need.items():
            self._wait(e, si, v)

    def _record(self, ev, reads, writes):
        for b in reads:
            b.r.append(ev)
            if len(b.r) > 16:
                d = {}
                for si, v in b.r:
                    if d.get(si, 0) < v:
                        d[si] = v
                b.r = list(d.items())
        for b in writes:
            b.w = ev
            b.r = []

    def op(self, e, fn, reads=(), writes=()):
        self._deps(e, reads, writes)
        si = self.esem[e]
        ins = fn()
        ins.then_inc(self.sems[si], 1)
        self.cnt[si] += 1
        ev = (si, self.cnt[si])
        self._record(ev, reads, writes)
        self.ninst += 1
        return ev

    def dma(self, q, out, in_, reads=(), writes=(), **kw):
        self._deps(q, reads, writes)
        k = self.dnext
        self.dnext = (self.dnext + 1) % len(self.dsem)
        si = self.dsem[k]
        if self.cnt[si] > 0:
            self._wait(q, si, self.cnt[si])
        ins = self.eng[q].dma_start(out=out, in_=in_, **kw)
        ins.then_inc(self.sems[si], 16)
        self.cnt[si] += 16
        ev = (si, self.cnt[si])
        self._record(ev, reads, writes)
        self.ninst += 1
        return ev

    def barrier(self):
        for e in self.eng:
            for si in range(len(self.sems)):
                if self.cnt[si] > 0:
                    self._wait(e, si, self.cnt[si])


class TPool:
    def __init__(self, alloc, name, shape, dtype, n):
        self.t = [(alloc(name + "_%d" % i, shape, dtype), Buf(name + "_%d" % i)) for i in range(n)]
        self.i = 0

    def get(self):
        r = self.t[self.i % len(self.t)]
        self.i += 1
        return r


def chunks_of(TC, T, size=512):
    out = []
    t = 0
    while t < TC:
        n = min(size, TC - t)
        out.append((t, n, True))
        t += n
    t = TC
    while t < TC + T:
        n = min(size, TC + T - t)
        out.append((t, n, False))
        t += n
    return out


def build(cfg, stop_after=None, final_norm=True, dbg=()):
    NB, T, TC, DEPTH, DFF = cfg["NB"], cfg["T"], cfg["TC"], cfg["DEPTH"], cfg["DFF"]
    TT = T + TC
    FT = DFF // 128
    assert TC % 128 == 0 and T % 128 == 0 and DFF % 128 == 0
    nc = bass.Bass("TRN2", target_bir_lowering=False)
    L = DEPTH

    def din(name, shape):
        return nc.dram_tensor(name, list(shape), F32, kind="ExternalInput").ap()

    I = {}
    I["x"] = din("x", [NB, T, D])
    I["c"] = din("c", [NB, D])
    I["ctx"] = din("ctx", [NB, TC, D])
    I["c_ctx"] = din("c_ctx", [D])
    I["ada_w"] = din("ada_w", [L, D, NMOD * D])
    I["ada_b"] = din("ada_b", [L, NMOD * D])
    I["norm_g"] = din("norm_g", [L, 3, D])
    I["ffn_w1"] = din("ffn_w1", [L, 2, 2 * FT, 128, DTL, 128])
    I["ffn_w2"] = din("ffn_w2", [L, 2, DTL, 128, FT, 128])
    I["w_in"] = din("w_in", [L, INC // 128, 128, DTL, 128])
    I["w_out"] = din("w_out", [L, DTL, 128, DTL, 128])
    I["s5_lam_re"] = din("s5_lam_re", [L, 2, 32, 64])
    I["s5_lam_im"] = din("s5_lam_im", [L, 2, 32, 64])
    I["s5_log_dt"] = din("s5_log_dt", [L, 2, 32])
    I["s5_b_re"] = din("s5_b_re", [L, 2, 32, 64, 16])
    I["s5_b_im"] = din("s5_b_im", [L, 2, 32, 64, 16])
    I["s5_c_re"] = din("s5_c_re", [L, 2, 32, 16, 64])
    I["s5_c_im"] = din("s5_c_im", [L, 2, 32, 16, 64])
    I["s5_d"] = din("s5_d", [L, S5W])
    I["s5_glu_w"] = din("s5_glu_w", [L, S5W, S5W])
    I["s5_glu_b"] = din("s5_glu_b", [L, S5W])
    I["rwkv_mu"] = din("rwkv_mu", [L, 6, RW])
    I["rwkv_w0"] = din("rwkv_w0", [L, 2, RW])
    I["rwkv_w1"] = din("rwkv_w1", [L, 2, RW, 64])
    I["rwkv_w2"] = din("rwkv_w2", [L, 2, 64, RW])
    I["rwkv_a0"] = din("rwkv_a0", [L, 2, RW])
    I["rwkv_a1"] = din("rwkv_a1", [L, 2, RW, 64])
    I["rwkv_a2"] = din("rwkv_a2", [L, 2, 64, RW])
    I["rwkv_g1"] = din("rwkv_g1", [L, RW, 224])
    I["rwkv_g2"] = din("rwkv_g2", [L, 224, RW])
    I["rwkv_k_k"] = din("rwkv_k_k", [L, RW])
    I["rwkv_k_a"] = din("rwkv_k_a", [L, RW])
    I["rwkv_r_k"] = din("rwkv_r_k", [L, NH, 64])
    I["rwkv_ln_w"] = din("rwkv_ln_w", [L, RW])
    I["rwkv_ln_b"] = din("rwkv_ln_b", [L, RW])
    LM = max(L - 1, 1)
    I["rwkv_mu_m"] = din("rwkv_mu_m", [LM, RW])
    I["rwkv_v0"] = din("rwkv_v0", [LM, RW])
    I["rwkv_v1"] = din("rwkv_v1", [LM, RW, 64])
    I["rwkv_v2"] = din("rwkv_v2", [LM, 64, RW])
    I["final_g"] = din("final_g", [D])
    I["k_ident"] = din("k_ident", [128, 128])
    I["k_ones"] = din("k_ones", [128, 128])
    I["k_bd64"] = din("k_bd64", [128, 128])
    I["k_mu2"] = din("k_mu2", [128, 256])
    I["k_ml2"] = din("k_ml2", [128, 256])
    I["k_anti"] = din("k_anti", [128, 128])
    out_ap = nc.dram_tensor("out", [NB, T, D], F32, kind="ExternalOutput").ap()

    def dscr(name, shape, dt=F32):
        kind = "ExternalOutput" if name in dbg else "Internal"
        return nc.dram_tensor(name, list(shape), dt, kind=kind).ap()

    XT = dscr("XT", [NB, D, TT])
    XT_b = Buf("XT")
    PT = dscr("PT", [NB, INC, TT])
    PT_b = Buf("PT")
    MIX = dscr("MIXs", [NB, D, TT])
    MIX_b = Buf("MIX")

    k = K(nc)
    OUT_b = Buf("out")

    def salloc(name, shape, dt=F32):
        return nc.alloc_sbuf_tensor(name, list(shape), dt)

    ident = salloc("ident", [128, 128]); ident_b = Buf("ident")
    ones = salloc("ones", [128, 128]); ones_b = Buf("ones")
    bd64 = salloc("bd64", [128, 128]); bd64_b = Buf("bd64")
    k.dma("sp", ident[:], I["k_ident"], writes=[ident_b])
    k.dma("sp", ones[:], I["k_ones"], writes=[ones_b])
    k.dma("sp", bd64[:], I["k_bd64"], writes=[bd64_b])
    NR = NB + 1
    SC = salloc("SC", [128, DTL, NR]); SC_b = Buf("SC")
    NPV = 144 + 48 + 8 + 72 + 24 + 24 + 12 * 7
    PV = salloc("PV", [128, NPV]); PV_b = Buf("PV")
    O_ADAB, O_NG, O_S5D, O_GLUB, O_MU, O_W0, O_A0 = 0, 144, 192, 196, 200, 272, 296
    O_KK, O_KA, O_RK, O_LNW, O_LNB, O_MUM, O_V0 = 320, 332, 344, 356, 368, 380, 392
    MOD = salloc("MOD", [128, NMOD, DTL, NR]); MOD_b = Buf("MOD")
    MA = salloc("MA", [128, 3, DTL, NR]); MA_b = Buf("MA")
    MG = salloc("MG", [128, 3, DTL, NR]); MG_b = Buf("MG")
    FG = salloc("FG", [128, DTL]); FG_b = Buf("FG")
    epsT = salloc("epsT", [128, 1]); eps_b = Buf("eps")
    k.op("dve", lambda: nc.vector.memset(epsT[:], NORM_EPS), writes=[eps_b])

    PS = TPool(lambda n, s, d: nc.alloc_psum_tensor(n, s, d), "ps", [128, 512], F32, 8)

    def transpose_rows_to(dst_ap, dst_buf, src_rows_ap, nrows, stage_pool):
        st, st_b = stage_pool.get()
        k.dma("sp", st[:nrows, :], src_rows_ap, writes=[st_b])
        ps, ps_b = PS.get()
        k.op("pe", lambda: nc.tensor.transpose(out=ps[:, :nrows], in_=st[:nrows, :], identity=ident[:nrows, :nrows]),
             reads=[st_b, ident_b], writes=[ps_b])
        k.op("dve", lambda: nc.vector.tensor_copy(out=dst_ap, in_=ps[:, :nrows]), reads=[ps_b], writes=[dst_buf])

    with ExitStack() as es:
        def talloc(name, shape, dt=F32):
            return es.enter_context(nc.sbuf_tensor(_u(name), list(shape), dt))
        stg = TPool(talloc, "p0stg", [128, 128], F32, 2)
        crow = talloc("crow", [NR, D]); crow_b = Buf("crow")
        k.dma("sp", crow[0:NB, :], I["c"], writes=[crow_b])
        k.dma("sp", crow[NB:NR, :], I["c_ctx"].rearrange("(o d) -> o d", o=1), writes=[crow_b])
        k.op("act", lambda: nc.scalar.activation(out=crow[:], in_=crow[:], func=AF.Silu), reads=[crow_b], writes=[crow_b])
        ps, ps_b = PS.get()
        for dt in range(DTL):
            k.op("pe", lambda dt=dt: nc.tensor.transpose(out=ps[:, dt * NR:(dt + 1) * NR], in_=crow[:NR, dt * 128:(dt + 1) * 128],
                                                       identity=ident[:NR, :NR]), reads=[crow_b, ident_b], writes=[ps_b])
        k.op("dve", lambda: nc.vector.tensor_copy(out=SC[:].rearrange("p a b -> p (a b)"), in_=ps[:, :DTL * NR]),
             reads=[ps_b], writes=[SC_b])
        transpose_rows_to(FG[:, :], FG_b, I["final_g"].rearrange("(a p) -> a p", p=128), DTL, stg)
        xin = TPool(talloc, "xin", [128, D], F32, 2)
        xo = TPool(talloc, "xo", [128, DTL, 128], F32, 2)
        for b in range(NB):
            for tt in range(TT // 128):
                xi, xi_b = xin.get()
                if tt * 128 < TC:
                    src = I["ctx"][b, tt * 128:(tt + 1) * 128, :]
                else:
                    src = I["x"][b, tt * 128 - TC:(tt + 1) * 128 - TC, :]
                k.dma("sp", xi[:], src, writes=[xi_b])
                xt_, xt_b = xo.get()
                for q in range(4):
                    ps, ps_b = PS.get()
                    for j in range(4):
                        dt = q * 4 + j
                        k.op("pe", lambda dt=dt, j=j, ps=ps, xi=xi: nc.tensor.transpose(
                            out=ps[:, j * 128:(j + 1) * 128], in_=xi[:, dt * 128:(dt + 1) * 128], identity=ident[:]),
                            reads=[xi_b, ident_b], writes=[ps_b])
                    eng = "dve" if q % 2 == 0 else "act"
                    if eng == "dve":
                        k.op("dve", lambda q=q, ps=ps, xt_=xt_: nc.vector.tensor_copy(
                            out=xt_[:, q * 4:(q + 1) * 4, :].rearrange("p a b -> p (a b)"), in_=ps[:]), reads=[ps_b], writes=[xt_b])
                    else:
                        k.op("act", lambda q=q, ps=ps, xt_=xt_: nc.scalar.copy(
                            out=xt_[:, q * 4:(q + 1) * 4, :].rearrange("p a b -> p (a b)"), in_=ps[:]), reads=[ps_b], writes=[xt_b])
                k.dma("pool", XT[b].rearrange("(dt p) t -> p dt t", p=128)[:, :, tt * 128:(tt + 1) * 128], xt_[:],
                      reads=[xt_b], writes=[XT_b])
        k.barrier()

    def load_params(layer, es):
        def talloc(name, shape, dt=F32):
            return es.enter_context(nc.sbuf_tensor(_u(name), list(shape), dt))
        stg = TPool(talloc, "pvstg%d" % layer, [128, 128], F32, 2)
        items = [
            (O_ADAB, I["ada_b"][layer].rearrange("(a p) -> a p", p=128), 144),
            (O_NG, I["norm_g"][layer].rearrange("s (a p) -> (s a) p", p=128), 48),
            (O_S5D, I["s5_d"][layer].rearrange("(a p) -> a p", p=128), 4),
            (O_GLUB, I["s5_glu_b"][layer].rearrange("(a p) -> a p", p=128), 4),
            (O_MU, I["rwkv_mu"][layer].rearrange("s (a p) -> (s a) p", p=128), 72),
            (O_W0, I["rwkv_w0"][layer].rearrange("s (a p) -> (s a) p", p=128), 24),
            (O_A0, I["rwkv_a0"][layer].rearrange("s (a p) -> (s a) p", p=128), 24),
            (O_KK, I["rwkv_k_k"][layer].rearrange("(a p) -> a p", p=128), 12),
            (O_KA, I["rwkv_k_a"][layer].rearrange("(a p) -> a p", p=128), 12),
            (O_RK, I["rwkv_r_k"][layer].rearrange("(a h) c -> a (h c)", h=2), 12),
            (O_LNW, I["rwkv_ln_w"][layer].rearrange("(a p) -> a p", p=128), 12),
            (O_LNB, I["rwkv_ln_b"][layer].rearrange("(a p) -> a p", p=128), 12),
        ]
        if layer > 0:
            items += [
                (O_MUM, I["rwkv_mu_m"][layer - 1].rearrange("(a p) -> a p", p=128), 12),
                (O_V0, I["rwkv_v0"][layer - 1].rearrange("(a p) -> a p", p=128), 12),
            ]
        for off, src, n in items:
            r0 = 0
            while r0 < n:
                m = min(128, n - r0)
                transpose_rows_to(PV[:, off + r0:off + r0 + m], PV_b, src[r0:r0 + m, :], m, stg)
                r0 += m

    def compute_mod(layer, es):
        def talloc(name, shape, dt=F32):
            return es.enter_context(nc.sbuf_tensor(_u(name), list(shape), dt))
        wst = TPool(talloc, "adaw%d" % layer, [128, 2048], F32, 3)
        for m in range(NMOD):
            ps, ps_b = PS.get()
            for kt in range(DTL):
                w, w_b = wst.get()
                k.dma("sp", w[:], I["ada_w"][layer, kt * 128:(kt + 1) * 128, m * D:(m + 1) * D], writes=[w_b])
                for ct in range(DTL):
                    k.op("pe", lambda ct=ct, kt=kt, w=w, ps=ps: nc.tensor.matmul(
                        out=ps[:, ct * NR:(ct + 1) * NR], lhsT=w[:, ct * 128:(ct + 1) * 128], rhs=SC[:, kt, :],
                        start=(kt == 0 and ct == 0), stop=(kt == DTL - 1 and ct == DTL - 1), skip_group_check=True),
                        reads=[w_b, SC_b], writes=[ps_b])
            k.op("dve", lambda m=m, ps=ps: nc.vector.tensor_tensor(
                out=MOD[:, m, :, :], in0=ps[:, :DTL * NR].rearrange("p (a b) -> p a b", b=NR),
                in1=PV[:, O_ADAB + m * DTL:O_ADAB + (m + 1) * DTL][:, :, None].to_broadcast([128, DTL, NR]), op=ALU.add),
                reads=[ps_b, PV_b], writes=[MOD_b])
        for s in range(3):
            k.op("dve", lambda s=s: nc.vector.scalar_tensor_tensor(
                out=MA[:, s, :, :], in0=MOD[:, 3 * s + 1, :, :], scalar=1.0,
                in1=PV[:, O_NG + s * DTL:O_NG + (s + 1) * DTL][:, :, None].to_broadcast([128, DTL, NR]),
                op0=ALU.add, op1=ALU.mult), reads=[MOD_b, PV_b], writes=[MA_b])
            gs = 1.0 if s == 1 else 0.5
            k.op("dve", lambda s=s, gs=gs: nc.vector.tensor_scalar(
                out=MG[:, s, :, :], in0=MOD[:, 3 * s + 2, :, :], scalar1=gs, scalar2=None, op0=ALU.mult),
                reads=[MOD_b], writes=[MG_b])

    class DensePhase:
        def __init__(self, es, tag, kt_max):
            def talloc(name, shape, dt=F32):
                return es.enter_context(nc.sbuf_tensor(_u(name), list(shape), dt))
            self.xs = TPool(talloc, tag + "xs", [128, DTL, 512], F32, 1)
            self.hb = TPool(talloc, tag + "hb", [128, DTL, 512], BF16, 1)
            self.tmp = TPool(talloc, tag + "tmp", [128, 512], F32, 4)
            self.rstd = TPool(talloc, tag + "rstd", [128, 512], F32, 1)
            self.wst = TPool(talloc, tag + "wst", [128, kt_max, 128], F32, 2)
            self.wbf = TPool(talloc, tag + "wbf", [128, kt_max, 128], BF16, 2)
            self.ncast = 0

        def load_x(self, b, t0, n):
            xs, xs_b = self.xs.get()
            k.dma("sp", xs[:, :, :n], XT[b].rearrange("(dt p) t -> p dt t", p=128)[:, :, t0:t0 + n], reads=[XT_b], writes=[xs_b])
            return xs, xs_b

        def store_x(self, xs, xs_b, b, t0, n):
            k.dma("pool", XT[b].rearrange("(dt p) t -> p dt t", p=128)[:, :, t0:t0 + n], xs[:, :, :n], reads=[xs_b], writes=[XT_b])

        def norm_mod(self, xs, xs_b, n, s, col):
            ps, ps_b = PS.get()
            for dt in range(DTL):
                sq, sq_b = self.tmp.get()
                k.op("act", lambda dt=dt, sq=sq: nc.scalar.activation(out=sq[:, :n], in_=xs[:, dt, :n], func=AF.Square),
                     reads=[xs_b], writes=[sq_b])
                k.op("pe", lambda dt=dt, sq=sq: nc.tensor.matmul(out=ps[:, :n], lhsT=ones[:], rhs=sq[:, :n],
                                                               start=(dt == 0), stop=(dt == DTL - 1)),
                     reads=[sq_b, ones_b], writes=[ps_b])
            rs, rs_b = self.rstd.get()
            k.op("act", lambda: nc.scalar.activation(out=rs[:, :n], in_=ps[:, :n], func=AF.Sqrt, scale=1.0 / D, bias=epsT[:, 0:1]),
                 reads=[ps_b, eps_b], writes=[rs_b])
            k.op("dve", lambda: nc.vector.reciprocal(out=rs[:, :n], in_=rs[:, :n]), reads=[rs_b], writes=[rs_b])
            hb, hb_b = self.hb.get()
            for dt in range(DTL):
                tm, tm_b = self.tmp.get()
                k.op("dve", lambda dt=dt, tm=tm: nc.vector.tensor_tensor(out=tm[:, :n], in0=xs[:, dt, :n], in1=rs[:, :n], op=ALU.mult),
                     reads=[xs_b, rs_b], writes=[tm_b])
                k.op("act", lambda dt=dt, tm=tm: nc.scalar.activation(
                    out=hb[:, dt, :n], in_=tm[:, :n], func=AF.Identity,
                    scale=MA[:, s, dt, col:col + 1], bias=MOD[:, 3 * s, dt, col:col + 1]),
                    reads=[tm_b, MA_b, MOD_b], writes=[hb_b])
            return hb, hb_b

        def load_w(self, wview, KT, c0):
            w, w_b = self.wst.get()
            k.dma("sp", w[:, :KT, :], wview[c0 // 128], writes=[w_b])
            wb, wb_b = self.wbf.get()
            eng = "pool" if self.ncast % 2 == 0 else "act"
            self.ncast += 1
            if eng == "pool":
                k.op("pool", lambda: nc.gpsimd.tensor_copy(out=wb[:, :KT, :], in_=w[:, :KT, :]), reads=[w_b], writes=[wb_b])
            else:
                k.op("act", lambda: nc.scalar.copy(out=wb[:, :KT, :], in_=w[:, :KT, :]), reads=[w_b], writes=[wb_b])
            return wb, wb_b

        def mm(self, wb, wb_b, KT, rhs, rhs_b, n):
            ps, ps_b = PS.get()
            for kt in range(KT):
                k.op("pe", lambda kt=kt: nc.tensor.matmul(out=ps[:, :n], lhsT=wb[:, kt, :], rhs=rhs[:, kt, :n],
                                                          start=(kt == 0), stop=(kt == KT - 1)),
                     reads=[wb_b, rhs_b], writes=[ps_b])
            return ps, ps_b

    def ffn_phase(layer, which, s):
        with ExitStack() as es:
            def talloc(name, shape, dt=F32):
                return es.enter_context(nc.sbuf_tensor(_u(name), list(shape), dt))
            dp = DensePhase(es, "f%d%d" % (layer, which), max(FT, DTL))
            act_t = TPool(talloc, "fa", [128, FT, 512], BF16, 1)
            sg_t = TPool(talloc, "fsg", [128, 512], F32, 2)
            w1 = I["ffn_w1"][layer, which]
            w2 = I["ffn_w2"][layer, which]
            for b in range(NB):
                for (t0, n, is_ctx) in chunks_of(TC, T):
                    col = NB if is_ctx else b
                    xs, xs_b = dp.load_x(b, t0, n)
                    hb, hb_b = dp.norm_mod(xs, xs_b, n, s, col)
                    a, a_b = act_t.get()
                    for j in range(FT):
                        wg, wg_b = dp.load_w(w1, DTL, j * 128)
                        pg, pg_b = dp.mm(wg, wg_b, DTL, hb, hb_b, n)
                        wu, wu_b = dp.load_w(w1, DTL, DFF + j * 128)
                        pu, pu_b = dp.mm(wu, wu_b, DTL, hb, hb_b, n)
                        sg, sg_b = sg_t.get()
                        k.op("act", lambda: nc.scalar.activation(out=sg[:, :n], in_=pg[:, :n], func=AF.Silu),
                             reads=[pg_b], writes=[sg_b])
                        k.op("dve", lambda j=j: nc.vector.tensor_tensor(out=a[:, j, :n], in0=sg[:, :n], in1=pu[:, :n], op=ALU.mult),
                             reads=[sg_b, pu_b], writes=[a_b])
                    for dt in range(DTL):
                        wo, wo_b = dp.load_w(w2, FT, dt * 128)
                        po, po_b = dp.mm(wo, wo_b, FT, a, a_b, n)
                        k.op("dve", lambda dt=dt: nc.vector.scalar_tensor_tensor(
                            out=xs[:, dt, :n], in0=po[:, :n], scalar=MG[:, s, dt, col:col + 1], in1=xs[:, dt, :n],
                            op0=ALU.mult, op1=ALU.add), reads=[po_b, MG_b, xs_b], writes=[xs_b])
                    dp.store_x(xs, xs_b, b, t0, n)
            k.barrier()

    S5G = dscr("S5G", [NB, S5W, TT]); S5G_b = Buf("S5G")
    BBd = dscr("BBd", [2, 32, 3, 16, 64]); BBd_b = Buf("BBd")
    STR = dscr("STR", [NB, 9, TT, RW]); STR_b = Buf("STR")
    GB = dscr("GBs", [NB, 2, RW, TT]); GB_b = Buf("GB")
    VF = dscr("VF", [NB, RW, TT]); VF_b = Buf("VF")
    YD = dscr("YD", [NB, 2, TT, RW]); YD_b = Buf("YD")
    CHUNKED = cfg.get("SCAN", "chunk") == "chunk"
    FM = dscr("FMs", [NB, 9, RW, TT]); FM_b = Buf("FM")
    TCH = [(t0, min(512, TT - t0)) for t0 in range(0, TT, 512)]

    def evac(i, out, in_, reads, writes):
        if i % 2 == 0:
            k.op("dve", lambda: nc.vector.tensor_copy(out=out, in_=in_), reads=reads, writes=writes)
        else:
            k.op("act", lambda: nc.scalar.copy(out=out, in_=in_), reads=reads, writes=writes)

    def win_phase(layer):
        with ExitStack() as es:
            def talloc(name, shape, dt=F32):
                return es.enter_context(nc.sbuf_tensor(_u(name), list(shape), dt))
            dp = DensePhase(es, "wi%d" % layer, DTL)
            ost = TPool(talloc, "wio", [128, 512], F32, 3)
            wv = I["w_in"][layer]
            for b in range(NB):
                for (t0, n, is_ctx) in chunks_of(TC, T):
                    col = NB if is_ctx else b
                    xs, xs_b = dp.load_x(b, t0, n)
                    hb, hb_b = dp.norm_mod(xs, xs_b, n, 1, col)
                    for nt in range(INC // 128):
                        wb, wb_b = dp.load_w(wv, DTL, nt * 128)
                        ps, ps_b = dp.mm(wb, wb_b, DTL, hb, hb_b, n)
                        o, o_b = ost.get()
                        evac(nt, o[:, :n], ps[:, :n], [ps_b], [o_b])
                        k.dma("pool", PT[b, nt * 128:(nt + 1) * 128, t0:t0 + n], o[:, :n], reads=[o_b], writes=[PT_b])
            k.barrier()

    def wout_phase(layer):
        with ExitStack() as es:
            def talloc(name, shape, dt=F32):
                return es.enter_context(nc.sbuf_tensor(_u(name), list(shape), dt))
            dp = DensePhase(es, "wo%d" % layer, DTL)
            mx_t = TPool(talloc, "womx", [128, DTL, 512], F32, 1)
            wv = I["w_out"][layer]
            for b in range(NB):
                for (t0, n, is_ctx) in chunks_of(TC, T):
                    col = NB if is_ctx else b
                    xs, xs_b = dp.load_x(b, t0, n)
                    mx, mx_b = mx_t.get()
                    k.dma("sp", mx[:, :, :n], MIX[b].rearrange("(dt p) t -> p dt t", p=128)[:, :, t0:t0 + n], reads=[MIX_b], writes=[mx_b])
                    hb, hb_b = dp.hb.get()
                    for dt in range(DTL):
                        evac(dt, hb[:, dt, :n], mx[:, dt, :n], [mx_b], [hb_b])
                    for dt in range(DTL):
                        wb, wb_b = dp.load_w(wv, DTL, dt * 128)
                        ps, ps_b = dp.mm(wb, wb_b, DTL, hb, hb_b, n)
                        k.op("dve", lambda dt=dt: nc.vector.scalar_tensor_tensor(
                            out=xs[:, dt, :n], in0=ps[:, :n], scalar=MG[:, 1, dt, col:col + 1], in1=xs[:, dt, :n],
                            op0=ALU.mult, op1=ALU.add), reads=[ps_b, MG_b, xs_b], writes=[xs_b])
                    dp.store_x(xs, xs_b, b, t0, n)
            k.barrier()

    TWO_PI = 2.0 * math.pi

    def s5_phase(layer):
        with ExitStack() as es:
            def talloc(name, shape, dt=F32):
                return es.enter_context(nc.sbuf_tensor(_u(name), list(shape), dt))
            lr = talloc("s5lr", [64, 64]); li = talloc("s5li", [64, 64]); dtt = talloc("s5dt", [64, 1])
            lr_b, li_b, dt_b = Buf(), Buf(), Buf()
            k.dma("sp", lr[:], I["s5_lam_re"][layer].rearrange("d g p -> (d g) p"), writes=[lr_b])
            k.dma("sp", li[:], I["s5_lam_im"][layer].rearrange("d g p -> (d g) p"), writes=[li_b])
            k.dma("sp", dtt[:], I["s5_log_dt"][layer].rearrange("d (g o) -> (d g) o", o=1), writes=[dt_b])
            k.op("act", lambda: nc.scalar.activation(out=dtt[:], in_=dtt[:], func=AF.Exp), reads=[dt_b], writes=[dt_b])
            W = {}
            for nm in ("xr", "xi", "mag", "y", "yf", "cs", "sn", "ar", "ai", "den", "t1", "t2", "fr", "fi", "nfr"):
                W[nm] = (talloc("s5w_" + nm, [64, 64]), Buf())
            yi_t = talloc("s5yi", [64, 64], mybir.dt.int32); yi_b = Buf()

            def w2(nm):
                return W[nm][0], W[nm][1]

            def vop(fn, reads, writes):
                k.op("dve", fn, reads=reads, writes=writes)
            xr, xr_b = w2("xr"); xi, xi_b = w2("xi"); mag, mag_b = w2("mag")
            vop(lambda: nc.vector.tensor_scalar(out=xr[:], in0=lr[:], scalar1=dtt[:, 0:1], scalar2=None, op0=ALU.mult), [lr_b, dt_b], [xr_b])
            vop(lambda: nc.vector.tensor_scalar(out=xi[:], in0=li[:], scalar1=dtt[:, 0:1], scalar2=None, op0=ALU.mult), [li_b, dt_b], [xi_b])
            k.op("act", lambda: nc.scalar.activation(out=mag[:], in_=xr[:], func=AF.Exp), reads=[xr_b], writes=[mag_b])

            def sin_of(dst, dst_b, shift_turns):
                y, y_b = w2("y"); yf, yf_b = w2("yf"); t1, t1_b = w2("t1")
                vop(lambda: nc.vector.tensor_scalar(out=y[:], in0=xi[:], scalar1=1.0 / TWO_PI, scalar2=shift_turns, op0=ALU.mult, op1=ALU.add), [xi_b], [y_b])
                vop(lambda: nc.vector.tensor_copy(out=yi_t[:], in_=y[:]), [y_b], [yi_b])
                vop(lambda: nc.vector.tensor_copy(out=yf[:], in_=yi_t[:]), [yi_b], [yf_b])
                vop(lambda: nc.vector.tensor_tensor(out=yf[:], in0=y[:], in1=yf[:], op=ALU.subtract), [y_b, yf_b], [yf_b])
                vop(lambda: nc.vector.tensor_scalar(out=t1[:], in0=yf[:], scalar1=0.5, scalar2=None, op0=ALU.is_gt), [yf_b], [t1_b])
                vop(lambda: nc.vector.tensor_tensor(out=yf[:], in0=yf[:], in1=t1[:], op=ALU.subtract), [yf_b, t1_b], [yf_b])
                vop(lambda: nc.vector.tensor_scalar(out=t1[:], in0=yf[:], scalar1=-0.5, scalar2=None, op0=ALU.is_lt), [yf_b], [t1_b])
                vop(lambda: nc.vector.tensor_tensor(out=yf[:], in0=yf[:], in1=t1[:], op=ALU.add), [yf_b, t1_b], [yf_b])
                vop(lambda: nc.vector.tensor_scalar(out=yf[:], in0=yf[:], scalar1=TWO_PI, scalar2=math.pi, op0=ALU.mult, op1=ALU.min), [yf_b], [yf_b])
                vop(lambda: nc.vector.tensor_scalar(out=yf[:], in0=yf[:], scalar1=-math.pi, scalar2=None, op0=ALU.max), [yf_b], [yf_b])
                k.op("act", lambda: nc.scalar.activation(out=dst[:], in_=yf[:], func=AF.Sin), reads=[yf_b], writes=[dst_b])
            cs, cs_b = w2("cs"); sn, sn_b = w2("sn")
            sin_of(sn, sn_b, 0.0)
            sin_of(cs, cs_b, 0.25)
            ar, ar_b = w2("ar"); ai, ai_b = w2("ai"); den, den_b = w2("den"); t1, t1_b = w2("t1"); t2, t2_b = w2("t2")
            fr, fr_b = w2("fr"); fi, fi_b = w2("fi"); nfr, nfr_b = w2("nfr")
            vop(lambda: nc.vector.tensor_tensor(out=ar[:], in0=mag[:], in1=cs[:], op=ALU.mult), [mag_b, cs_b], [ar_b])
            vop(lambda: nc.vector.tensor_tensor(out=ai[:], in0=mag[:], in1=sn[:], op=ALU.mult), [mag_b, sn_b], [ai_b])
            vop(lambda: nc.vector.tensor_tensor(out=den[:], in0=lr[:], in1=lr[:], op=ALU.mult), [lr_b], [den_b])
            vop(lambda: nc.vector.tensor_tensor(out=t1[:], in0=li[:], in1=li[:], op=ALU.mult), [li_b], [t1_b])
            vop(lambda: nc.vector.tensor_tensor(out=den[:], in0=den[:], in1=t1[:], op=ALU.add), [den_b, t1_b], [den_b])
            vop(lambda: nc.vector.reciprocal(out=den[:], in_=den[:]), [den_b], [den_b])
            vop(lambda: nc.vector.tensor_scalar(out=t2[:], in0=ar[:], scalar1=-1.0, scalar2=None, op0=ALU.add), [ar_b], [t2_b])
            vop(lambda: nc.vector.tensor_tensor(out=fr[:], in0=t2[:], in1=lr[:], op=ALU.mult), [t2_b, lr_b], [fr_b])
            vop(lambda: nc.vector.tensor_tensor(out=t1[:], in0=ai[:], in1=li[:], op=ALU.mult), [ai_b, li_b], [t1_b])
            vop(lambda: nc.vector.tensor_tensor(out=fr[:], in0=fr[:], in1=t1[:], op=ALU.add), [fr_b, t1_b], [fr_b])
            vop(lambda: nc.vector.tensor_tensor(out=fr[:], in0=fr[:], in1=den[:], op=ALU.mult), [fr_b, den_b], [fr_b])
            vop(lambda: nc.vector.tensor_tensor(out=fi[:], in0=ai[:], in1=lr[:], op=ALU.mult), [ai_b, lr_b], [fi_b])
            vop(lambda: nc.vector.tensor_tensor(out=t1[:], in0=t2[:], in1=li[:], op=ALU.mult), [t2_b, li_b], [t1_b])
            vop(lambda: nc.vector.tensor_tensor(out=fi[:], in0=fi[:], in1=t1[:], op=ALU.subtract), [fi_b, t1_b], [fi_b])
            vop(lambda: nc.vector.tensor_tensor(out=fi[:], in0=fi[:], in1=den[:], op=ALU.mult), [fi_b, den_b], [fi_b])
            bre = talloc("s5bre", [64, 64, 16]); bim = talloc("s5bim", [64, 64, 16]); bre_b, bim_b = Buf(), Buf()
            k.dma("sp", bre[:], I["s5_b_re"][layer].rearrange("d g p h -> (d g) p h"), writes=[bre_b])
            k.dma("sp", bim[:], I["s5_b_im"][layer].rearrange("d g p h -> (d g) p h"), writes=[bim_b])
            BB = talloc("s5BB", [64, 3, 16, 64]); BB_b = Buf()
            tb1 = talloc("s5tb1", [64, 64, 16]); tb1_b = Buf()
            frb = fr[:, :, None].to_broadcast([64, 64, 16]); fib = fi[:, :, None].to_broadcast([64, 64, 16])

            def bbv(c):
                return BB[:, c, :, :].rearrange("q h p -> q p h")
            vop(lambda: nc.vector.tensor_tensor(out=bbv(0), in0=bre[:], in1=frb, op=ALU.mult), [bre_b, fr_b], [BB_b])
            vop(lambda: nc.vector.tensor_tensor(out=tb1[:], in0=bim[:], in1=fib, op=ALU.mult), [bim_b, fi_b], [tb1_b])
            vop(lambda: nc.vector.tensor_tensor(out=bbv(0), in0=bbv(0), in1=tb1[:], op=ALU.subtract), [BB_b, tb1_b], [BB_b])
            vop(lambda: nc.vector.tensor_tensor(out=bbv(1), in0=bim[:], in1=frb, op=ALU.mult), [bim_b, fr_b], [BB_b])
            vop(lambda: nc.vector.tensor_tensor(out=tb1[:], in0=bre[:], in1=fib, op=ALU.mult), [bre_b, fi_b], [tb1_b])
            vop(lambda: nc.vector.tensor_tensor(out=bbv(1), in0=bbv(1), in1=tb1[:], op=ALU.add), [BB_b, tb1_b], [BB_b])
            vop(lambda: nc.vector.tensor_scalar(out=BB[:, 2, :, :], in0=BB[:, 0, :, :], scalar1=-1.0, scalar2=None, op0=ALU.mult), [BB_b], [BB_b])
            k.dma("pool", BBd.rearrange("d g c h p -> (d g) c h p"), BB[:], reads=[BB_b], writes=[BBd_b])
            dup = talloc("s5dup", [64, 128]); dup_b = Buf()
            MAGT = talloc("s5MAGT", [128, 64]); MAGT_b = Buf()
            NLV = max(1, (TT - 1).bit_length())
            WR = talloc("s5WR", [128, NLV, 64]); WI = talloc("s5WI", [128, NLV, 64]); WRI_b = Buf()
            for src, src_b, dst in ((mag, mag_b, MAGT[:, :]), (cs, cs_b, WR[:, 0, :]), (sn, sn_b, WI[:, 0, :])):
                vop(lambda: nc.vector.tensor_copy(out=dup[:, 0:64], in_=src[:]), [src_b], [dup_b])
                vop(lambda: nc.vector.tensor_copy(out=dup[:, 64:128], in_=src[:]), [src_b], [dup_b])
                ps, ps_b = PS.get()
                k.op("pe", lambda: nc.tensor.transpose(out=ps[:, :64], in_=dup[:, :], identity=ident[:64, :64]),
                     reads=[dup_b, ident_b], writes=[ps_b])
                dbuf = MAGT_b if dst is MAGT[:, :] else WRI_b
                vop(lambda: nc.vector.tensor_copy(out=dst, in_=ps[:, :64]), [ps_b], [MAGT_b, WRI_b])
            tq = talloc("s5tq", [128, 64]); tq_b = Buf()
            for j in range(1, NLV):
                vop(lambda: nc.vector.tensor_tensor(out=tq[:], in0=WI[:, j - 1, :], in1=WI[:, j - 1, :], op=ALU.mult), [WRI_b], [tq_b])
                vop(lambda: nc.vector.tensor_tensor(out=WR[:, j, :], in0=WR[:, j - 1, :], in1=WR[:, j - 1, :], op=ALU.mult), [WRI_b], [WRI_b])
                vop(lambda: nc.vector.tensor_tensor(out=WR[:, j, :], in0=WR[:, j, :], in1=tq[:], op=ALU.subtract), [WRI_b, tq_b], [WRI_b])
                vop(lambda: nc.vector.tensor_tensor(out=tq[:], in0=WR[:, j - 1, :], in1=WI[:, j - 1, :], op=ALU.mult), [WRI_b], [tq_b])
                vop(lambda: nc.vector.tensor_scalar(out=WI[:, j, :], in0=tq[:], scalar1=2.0, scalar2=None, op0=ALU.mult), [tq_b], [WRI_b])
            CT1 = talloc("s5CT1", [128, 8, 128]); CT2 = talloc("s5CT2", [128, 8, 128]); CT_b = Buf()
            cst = TPool(talloc, "s5cst", [128, 128], F32, 2)
            cre = I["s5_c_re"][layer].rearrange("d g h p -> (d g h) p")
            cim = I["s5_c_im"][layer].rearrange("d g h p -> (d g h) p")
            for rb in range(8):
                for var in range(2):
                    st, st_b = cst.get()
                    a_, b_ = (cre, cim) if var == 0 else (cim, cre)
                    k.dma("sp", st[:, 0:64], a_[rb * 128:(rb + 1) * 128, :], writes=[st_b])
                    k.dma("sp", st[:, 64:128], b_[rb * 128:(rb + 1) * 128, :], writes=[st_b])
                    ps, ps_b = PS.get()
                    k.op("pe", lambda: nc.tensor.transpose(out=ps[:, :128], in_=st[:, :], identity=ident[:]),
                         reads=[st_b, ident_b], writes=[ps_b])
                    if var == 0:
                        vop(lambda: nc.vector.tensor_copy(out=CT1[0:64, rb, :], in_=ps[0:64, :128]), [ps_b], [CT_b])
                        vop(lambda: nc.vector.tensor_scalar(out=CT1[64:128, rb, :], in0=ps[64:128, :128], scalar1=-1.0, scalar2=None, op0=ALU.mult), [ps_b], [CT_b])
                    else:
                        vop(lambda: nc.vector.tensor_scalar(out=CT2[:, rb, :], in0=ps[:, :128], scalar1=-1.0, scalar2=None, op0=ALU.mult), [ps_b], [CT_b])
            UT = [(talloc("s5UT%d" % b, [128, TT]), Buf()) for b in range(NB)]
            UR = [(talloc("s5UR%d" % b, [128, TT]), Buf()) for b in range(NB)]
            YA = [(talloc("s5YA%d" % b, [128, TT]), Buf()) for b in range(NB)]
            ER = talloc("s5ER", [128, TT]); EI = talloc("s5EI", [128, TT]); E_b = Buf()
            et = talloc("s5et", [128, 1 << (NLV - 1)]); et_b = Buf()
            XM = talloc("s5XM", [128, TT]); XM_b = Buf()
            Qt = talloc("s5Q", [128, TT]); Q_b = Buf()
            Q1 = talloc("s5Q1", [128, TT]); Q1_b = Buf()
            Q2 = talloc("s5Q2", [128, TT]); Q2_b = Buf()
            tm_t = TPool(talloc, "s5tm", [128, 512], F32, 3)
            LB_t = TPool(talloc, "s5LB", [128, 2, 128], F32, 2)
            CP_t = TPool(talloc, "s5CP", [128, 2, 128], F32, 2)
            for c4 in range(4):
                for b in range(NB):
                    ut, ut_b = UT[b]; ur, ur_b = UR[b]; ya, ya_b = YA[b]
                    k.dma("sp", ut[:], PT[b, c4 * 128:(c4 + 1) * 128, :], reads=[PT_b], writes=[ut_b])
                    vop(lambda: nc.vector.tensor_copy(out=ur[:, 0:TC], in_=ut[:, TC - 1::-1] if TC > 0 else ut[:, 0:0]), [ut_b], [ur_b])
                    vop(lambda: nc.vector.tensor_copy(out=ur[:, TC:TT], in_=ut[:, TT - 1:TC - 1:-1]), [ut_b], [ur_b])
                    k.op("act", lambda: nc.scalar.activation(out=ya[:], in_=ut[:], func=AF.Identity, scale=PV[:, O_S5D + c4:O_S5D + c4 + 1]),
                         reads=[ut_b, PV_b], writes=[ya_b])
                for d in range(2):
                    for g8 in range(8):
                        g = c4 * 8 + g8
                        dg = d * 32 + g
                        vop(lambda: nc.vector.memset(ER[:, 0:1], 1.0), [], [E_b])
                        vop(lambda: nc.vector.memset(EI[:, 0:1], 0.0), [], [E_b])
                        for j in range(NLV):
                            lo = 1 << j
                            m = min(lo, TT - lo)
                            if m <= 0:
                                break
                            wr = WR[:, j, dg:dg + 1]; wi = WI[:, j, dg:dg + 1]
                            vop(lambda: nc.vector.tensor_scalar(out=et[:, :m], in0=EI[:, 0:m], scalar1=wi, scalar2=None, op0=ALU.mult), [E_b, WRI_b], [et_b])
                            vop(lambda: nc.vector.scalar_tensor_tensor(out=ER[:, lo:lo + m], in0=ER[:, 0:m], scalar=wr, in1=et[:, :m],
                                                                        op0=ALU.mult, op1=ALU.subtract), [E_b, WRI_b, et_b], [E_b])
                            vop(lambda: nc.vector.tensor_scalar(out=et[:, :m], in0=EI[:, 0:m], scalar1=wr, scalar2=None, op0=ALU.mult), [E_b, WRI_b], [et_b])
                            vop(lambda: nc.vector.scalar_tensor_tensor(out=EI[:, lo:lo + m], in0=ER[:, 0:m], scalar=wi, in1=et[:, :m],
                                                                        op0=ALU.mult, op1=ALU.add), [E_b, WRI_b, et_b], [E_b])
                        LB, LB_b = LB_t.get()
                        k.op("pool", lambda: nc.gpsimd.memset(LB[:], 0.0), writes=[LB_b])
                        r0 = g8 * 16
                        k.dma("sp", LB[r0:r0 + 16, 0, :].rearrange("h (c p) -> h c p", c=2), BBd[d, g, 0:2].rearrange("c h p -> h c p"),
                              reads=[BBd_b], writes=[LB_b])
                        k.dma("sp", LB[r0:r0 + 16, 1, :].rearrange("h (c p) -> h c p", c=2), BBd[d, g, 1:3].rearrange("c h p -> h c p"),
                              reads=[BBd_b], writes=[LB_b])
                        CP, CP_b = CP_t.get()
                        k.op("pool", lambda: nc.gpsimd.memset(CP[:], 0.0), writes=[CP_b])
                        rb = dg // 8
                        cc = (dg % 8) * 16
                        vop(lambda: nc.vector.tensor_copy(out=CP[:, 0, r0:r0 + 16], in_=CT1[:, rb, cc:cc + 16]), [CT_b], [CP_b])
                        vop(lambda: nc.vector.tensor_copy(out=CP[:, 1, r0:r0 + 16], in_=CT2[:, rb, cc:cc + 16]), [CT_b], [CP_b])
                        for b in range(NB):
                            src, src_b = UT[b] if d == 0 else UR[b]
                            ya, ya_b = YA[b]
                            for (t0, n) in TCH:
                                p1, p1_b = PS.get()
                                k.op("pe", lambda: nc.tensor.matmul(out=p1[:, :n], lhsT=LB[:, 0, :], rhs=src[:, t0:t0 + n], start=True, stop=True),
                                     reads=[LB_b, src_b], writes=[p1_b])
                                p2, p2_b = PS.get()
                                k.op("pe", lambda: nc.tensor.matmul(out=p2[:, :n], lhsT=LB[:, 1, :], rhs=src[:, t0:t0 + n], start=True, stop=True),
                                     reads=[LB_b, src_b], writes=[p2_b])
                                ta, ta_b = tm_t.get()
                                vop(lambda: nc.vector.tensor_tensor(out=ta[:, :n], in0=p1[:, :n], in1=ER[:, t0:t0 + n], op=ALU.mult), [p1_b, E_b], [ta_b])
                                tb, tb_b = tm_t.get()
                                vop(lambda: nc.vector.tensor_tensor(out=tb[:, :n], in0=p2[:, :n], in1=EI[:, t0:t0 + n], op=ALU.mult), [p2_b, E_b], [tb_b])
                                k.op("pool", lambda: nc.gpsimd.tensor_tensor(out=XM[:, t0:t0 + n], in0=ta[:, :n], in1=tb[:, :n], op=ALU.add),
                                     reads=[ta_b, tb_b], writes=[XM_b])
                            vop(lambda: nc.vector.tensor_tensor_scan(out=Qt[:], data0=MAGT[:, dg:dg + 1].to_broadcast([128, TT]), data1=XM[:],
                                                                     initial=0.0, op0=ALU.mult, op1=ALU.add), [MAGT_b, XM_b], [Q_b])
                            k.op("pool", lambda: nc.gpsimd.tensor_tensor(out=Q1[:], in0=Qt[:], in1=ER[:], op=ALU.mult), reads=[Q_b, E_b], writes=[Q1_b])
                            k.op("pool", lambda: nc.gpsimd.tensor_tensor(out=Q2[:], in0=Qt[:], in1=EI[:], op=ALU.mult), reads=[Q_b, E_b], writes=[Q2_b])
                            for (t0, n) in TCH:
                                py, py_b = PS.get()
                                k.op("pe", lambda: nc.tensor.matmul(out=py[:, :n], lhsT=CP[:, 0, :], rhs=Q1[:, t0:t0 + n], start=True, stop=False),
                                     reads=[CP_b, Q1_b], writes=[py_b])
                                k.op("pe", lambda: nc.tensor.matmul(out=py[:, :n], lhsT=CP[:, 1, :], rhs=Q2[:, t0:t0 + n], start=False, stop=True),
                                     reads=[CP_b, Q2_b], writes=[py_b])
                                if d == 0:
                                    vop(lambda: nc.vector.tensor_tensor(out=ya[:, t0:t0 + n], in0=py[:, :n], in1=ya[:, t0:t0 + n], op=ALU.add), [py_b, ya_b], [ya_b])
                                else:
                                    segs = []
                                    if t0 < TC:
                                        segs.append((t0, min(t0 + n, TC), True))
                                    if t0 + n > TC:
                                        segs.append((max(t0, TC), t0 + n, False))
                                    for (v0, v1, isc) in segs:
                                        m = v1 - v0
                                        hi = (TC - 1 - v0) if isc else (TT - 1 - (v0 - TC))
                                        lo_ = hi - m
                                        if lo_ >= 0:
                                            dst = ya[:, hi:lo_:-1]
                                        else:
                                            dst = ya[:, hi::-1]
                                        vop(lambda: nc.vector.tensor_tensor(out=dst, in0=py[:, v0 - t0:v1 - t0], in1=dst, op=ALU.add), [py_b, ya_b], [ya_b])
                for b in range(NB):
                    ya, ya_b = YA[b]
                    k.op("act", lambda: nc.scalar.activation(out=Q1[:], in_=ya[:], func=AF.Square), reads=[ya_b], writes=[Q1_b])
                    vop(lambda: nc.vector.tensor_scalar(out=Q1[:], in0=Q1[:], scalar1=0.044715, scalar2=1.0, op0=ALU.mult, op1=ALU.add), [Q1_b], [Q1_b])
                    vop(lambda: nc.vector.tensor_tensor(out=Q1[:], in0=Q1[:], in1=ya[:], op=ALU.mult), [Q1_b, ya_b], [Q1_b])
                    k.op("act", lambda: nc.scalar.activation(out=Q1[:], in_=Q1[:], func=AF.Sigmoid, scale=1.5957691216057308), reads=[Q1_b], writes=[Q1_b])
                    vop(lambda: nc.vector.tensor_tensor(out=Q2[:], in0=Q1[:], in1=ya[:], op=ALU.mult), [Q1_b, ya_b], [Q2_b])
                    k.dma("pool", S5G[b, c4 * 128:(c4 + 1) * 128, :], Q2[:], reads=[Q2_b], writes=[S5G_b])
            k.barrier()
        with ExitStack() as es:
            def talloc(name, shape, dt=F32):
                return es.enter_context(nc.sbuf_tensor(_u(name), list(shape), dt))
            gw = talloc("s5gw", [128, 4, S5W]); gw_b = Buf()
            k.dma("sp", gw[:], I["s5_glu_w"][layer].rearrange("(kt p) n -> p kt n", p=128), writes=[gw_b])
            g_t = TPool(talloc, "s5g", [128, 4, 512], F32, 2)
            o_t = TPool(talloc, "s5o", [128, 512], F32, 3)
            for b in range(NB):
                for (t0, n) in TCH:
                    gt, gt_b = g_t.get()
                    k.dma("sp", gt[:, :, :n], S5G[b].rearrange("(kt p) t -> p kt t", p=128)[:, :, t0:t0 + n], reads=[S5G_b], writes=[gt_b])
                    for nt in range(4):
                        ps, ps_b = PS.get()
                        for kt in range(4):
                            k.op("pe", lambda: nc.tensor.matmul(out=ps[:, :n], lhsT=gw[:, kt, nt * 128:(nt + 1) * 128], rhs=gt[:, kt, :n],
                                                               start=(kt == 0), stop=(kt == 3)), reads=[gw_b, gt_b], writes=[ps_b])
                        o, o_b = o_t.get()
                        k.op("act", lambda: nc.scalar.activation(out=o[:, :n], in_=ps[:, :n], func=AF.Sigmoid,
                                                                bias=PV[:, O_GLUB + nt:O_GLUB + nt + 1]), reads=[ps_b, PV_b], writes=[o_b])
                        k.op("dve", lambda: nc.vector.tensor_tensor(out=o[:, :n], in0=o[:, :n], in1=gt[:, nt, :n], op=ALU.mult), reads=[o_b, gt_b], writes=[o_b])
                        k.dma("pool", MIX[b, nt * 128:(nt + 1) * 128, t0:t0 + n], o[:, :n], reads=[o_b], writes=[MIX_b])
            k.barrier()

    EM05 = math.exp(-0.5)

    def shift_diff(dst, dst_b, src, src_b, c, eng="dve"):
        def sub(o, a, b_):
            if eng == "dve":
                k.op("dve", lambda: nc.vector.tensor_tensor(out=o, in0=a, in1=b_, op=ALU.subtract), reads=[src_b], writes=[dst_b])
            else:
                k.op("pool", lambda: nc.gpsimd.tensor_tensor(out=o, in0=a, in1=b_, op=ALU.subtract), reads=[src_b], writes=[dst_b])

        def neg(o, a):
            if eng == "dve":
                k.op("dve", lambda: nc.vector.tensor_scalar(out=o, in0=a, scalar1=-1.0, scalar2=None, op0=ALU.mult), reads=[src_b], writes=[dst_b])
            else:
                k.op("pool", lambda: nc.gpsimd.tensor_scalar(out=o, in0=a, scalar1=-1.0, scalar2=None, op0=ALU.mult), reads=[src_b], writes=[dst_b])
        if c < RT // 2:
            sub(dst[:, 1:TC], src[:, 0:TC - 1], src[:, 1:TC])
            neg(dst[:, 0:1], src[:, 0:1])
        else:
            sub(dst[:, 0:TC - 1], src[:, 1:TC], src[:, 0:TC - 1])
            neg(dst[:, TC - 1:TC], src[:, TC - 1:TC])
        Rr = T // GRID
        sv = src[:, TC:TT].rearrange("p (r w) -> p r w", w=GRID)
        dv = dst[:, TC:TT].rearrange("p (r w) -> p r w", w=GRID)
        q = c // (RT // 4)
        if q == 0:
            sub(dv[:, :, 1:GRID], sv[:, :, 0:GRID - 1], sv[:, :, 1:GRID])
            neg(dv[:, :, 0:1], sv[:, :, 0:1])
        elif q == 1:
            sub(dv[:, :, 0:GRID - 1], sv[:, :, 1:GRID], sv[:, :, 0:GRID - 1])
            neg(dv[:, :, GRID - 1:GRID], sv[:, :, GRID - 1:GRID])
        elif q == 2:
            if Rr > 1:
                sub(dv[:, 1:Rr, :], sv[:, 0:Rr - 1, :], sv[:, 1:Rr, :])
            neg(dv[:, 0:1, :], sv[:, 0:1, :])
        else:
            if Rr > 1:
                sub(dv[:, 0:Rr - 1, :], sv[:, 1:Rr, :], sv[:, 0:Rr - 1, :])
            neg(dv[:, Rr - 1:Rr, :], sv[:, Rr - 1:Rr, :])

    def rwkv_prep(layer):
        with ExitStack() as es:
            def talloc(name, shape, dt=F32):
                return es.enter_context(nc.sbuf_tensor(_u(name), list(shape), dt))
            first = (layer == 0)
            lw_b = Buf()
            omka = talloc("romka", [128, RT]); omka_b = Buf()
            k.op("dve", lambda: nc.vector.tensor_scalar(out=omka[:], in0=PV[:, O_KA:O_KA + RT], scalar1=-1.0, scalar2=1.0, op0=ALU.mult, op1=ALU.add),
                 reads=[PV_b], writes=[omka_b])
            WACC = talloc("rWACC", [128, TT]); AACC = talloc("rAACC", [128, TT]); GACC = talloc("rGACC", [112, 2, TT])
            VACC = talloc("rVACC", [64, TT]) if not first else None
            acc_b = Buf()
            big = TPool(talloc, "rbig", [128, TT], F32, 5)

            def pass1(b):
              if True:
                e1 = ExitStack()

                def ta1(name, shape, dt=F32):
                    return e1.enter_context(nc.sbuf_tensor(_u(name), list(shape), dt))
                w1t = ta1("rw1", [128, RT, 128]); a1t = ta1("ra1", [128, RT, 128]); g1t = ta1("rg1", [128, RT, 224])
                for d in range(2):
                    k.dma("sp", w1t[:, :, d * 64:(d + 1) * 64], I["rwkv_w1"][layer, d].rearrange("(c p) l -> p c l", p=128), writes=[lw_b])
                    k.dma("sp", a1t[:, :, d * 64:(d + 1) * 64], I["rwkv_a1"][layer, d].rearrange("(c p) l -> p c l", p=128), writes=[lw_b])
                k.dma("sp", g1t[:], I["rwkv_g1"][layer].rearrange("(c p) l -> p c l", p=128), writes=[lw_b])
                if not first:
                    v1t = ta1("rv1", [128, RT, 64])
                    k.dma("sp", v1t[:], I["rwkv_v1"][layer - 1].rearrange("(c p) l -> p c l", p=128), writes=[lw_b])
                for c in range(RT):
                    Z, Z_b = big.get()
                    k.dma("sp", Z[:], PT[b, S5W + 3 * RW + c * 128:S5W + 3 * RW + (c + 1) * 128, :], reads=[PT_b], writes=[Z_b])
                    DZ, DZ_b = big.get()
                    shift_diff(DZ, DZ_b, Z, Z_b, c)
                    X, X_b = big.get()
                    jobs = [(3, w1t, WACC, 128, None), (4, a1t, AACC, 128, None), (5, g1t, GACC, 224, None)]
                    if not first:
                        jobs.append((-1, v1t, VACC, 64, None))
                    for (mi, wt, acc, width, _) in jobs:
                        if mi >= 0:
                            mcol = PV[:, O_MU + mi * RT + c:O_MU + mi * RT + c + 1]
                        else:
                            mcol = PV[:, O_MUM + c:O_MUM + c + 1]
                        k.op("dve", lambda: nc.vector.scalar_tensor_tensor(out=X[:], in0=DZ[:], scalar=mcol, in1=Z[:], op0=ALU.mult, op1=ALU.add),
                             reads=[DZ_b, Z_b, PV_b], writes=[X_b])
                        for (t0, n) in TCH:
                            nparts = [(0, 128)] if width == 128 else ([(0, 64)] if width == 64 else [(0, 112), (112, 112)])
                            for pi, (c0, m) in enumerate(nparts):
                                ps, ps_b = PS.get()
                                k.op("pe", lambda: nc.tensor.matmul(out=ps[:m, :n], lhsT=wt[:, c, c0:c0 + m], rhs=X[:, t0:t0 + n], start=True, stop=True),
                                     reads=[lw_b, X_b], writes=[ps_b])
                                dst = acc[:m, pi, t0:t0 + n] if width == 224 else acc[:m, t0:t0 + n]
                                if c == 0:
                                    k.op("act", lambda: nc.scalar.copy(out=dst, in_=ps[:m, :n]), reads=[ps_b], writes=[acc_b])
                                else:
                                    k.op("dve", lambda: nc.vector.tensor_tensor(out=dst, in0=ps[:m, :n], in1=dst, op=ALU.add), reads=[ps_b, acc_b], writes=[acc_b])
                k.op("act", lambda: nc.scalar.activation(out=WACC[:], in_=WACC[:], func=AF.Tanh), reads=[acc_b], writes=[acc_b])
                k.op("act", lambda: nc.scalar.activation(out=GACC[:], in_=GACC[:], func=AF.Sigmoid), reads=[acc_b], writes=[acc_b])
                k.barrier()
                e1.close()

            def pass2(b):
              if True:
                e2 = ExitStack()

                def ta2(name, shape, dt=F32):
                    return e2.enter_context(nc.sbuf_tensor(_u(name), list(shape), dt))
                w2t = ta2("rw2", [128, RW]); a2t = ta2("ra2", [128, RW]); g2t = ta2("rg2", [112, 2, RW])
                k.dma("sp", w2t[:], I["rwkv_w2"][layer].rearrange("d l c -> (d l) c"), writes=[lw_b])
                k.dma("sp", a2t[:], I["rwkv_a2"][layer].rearrange("d l c -> (d l) c"), writes=[lw_b])
                k.dma("sp", g2t[:], I["rwkv_g2"][layer].rearrange("(k p) c -> p k c", p=112), writes=[lw_b])
                if not first:
                    v2t = ta2("rv2", [64, RW])
                    k.dma("sp", v2t[:], I["rwkv_v2"][layer - 1], writes=[lw_b])
                ch = TPool(ta2, "rch", [128, 512], F32, 16)
                tk = TPool(ta2, "rtk", [128, 4, 128], F32, 4)
                for c in range(RT):
                    mixed = []
                    for si, mi in ((0, 0), (1, 1), (2, 2)):
                        Rw, Rw_b = big.get()
                        base = S5W + si * RW + c * 128
                        k.dma("sp", Rw[:], PT[b, base:base + 128, :], reads=[PT_b], writes=[Rw_b])
                        Dd, Dd_b = big.get()
                        shift_diff(Dd, Dd_b, Rw, Rw_b, c, eng="pool" if si == 1 else "dve")
                        mcol = PV[:, O_MU + mi * RT + c:O_MU + mi * RT + c + 1]
                        k.op("dve", lambda: nc.vector.scalar_tensor_tensor(out=Dd[:], in0=Dd[:], scalar=mcol, in1=Rw[:], op0=ALU.mult, op1=ALU.add),
                             reads=[Dd_b, Rw_b, PV_b], writes=[Dd_b])
                        mixed.append((Dd, Dd_b))
                    (RRt, RR_b), (KXt, KX_b), (VXt, VX_b) = mixed
                    if first:
                        k.dma("pool", VF[b, c * 128:(c + 1) * 128, :], VXt[:], reads=[VX_b], writes=[VF_b])
                    for (t0, n) in TCH:
                        sl = slice(t0, t0 + n)

                        def newt():
                            return ch.get()
                        streams = {}
                        kkc = PV[:, O_KK + c:O_KK + c + 1]
                        q2, q2_b = newt()
                        k.op("act", lambda: nc.scalar.activation(out=q2[:, :n], in_=KXt[:, sl], func=AF.Square, scale=kkc), reads=[KX_b, PV_b], writes=[q2_b])
                        pss, pss_b = PS.get()
                        k.op("pe", lambda: nc.tensor.matmul(out=pss[:, :n], lhsT=bd64[:], rhs=q2[:, :n], start=True, stop=True),
                             reads=[bd64_b, q2_b], writes=[pss_b])
                        rin, rin_b = newt()
                        k.op("act", lambda: nc.scalar.activation(out=rin[:, :n], in_=pss[:, :n], func=AF.Sqrt), reads=[pss_b], writes=[rin_b])
                        k.op("dve", lambda: nc.vector.tensor_scalar(out=rin[:, :n], in0=rin[:, :n], scalar1=1e-12, scalar2=None, op0=ALU.max), reads=[rin_b], writes=[rin_b])
                        k.op("dve", lambda: nc.vector.reciprocal(out=rin[:, :n], in_=rin[:, :n]), reads=[rin_b], writes=[rin_b])
                        kk, kk_b = newt()
                        k.op("dve", lambda: nc.vector.scalar_tensor_tensor(out=kk[:, :n], in0=KXt[:, sl], scalar=kkc, in1=rin[:, :n], op0=ALU.mult, op1=ALU.mult),
                             reads=[KX_b, PV_b, rin_b], writes=[kk_b])
                        streams[7] = (kk, kk_b)
                        kds = []
                        for d in range(2):
                            pw, pw_b = PS.get()
                            k.op("pe", lambda: nc.tensor.matmul(out=pw[:, :n], lhsT=w2t[d * 64:(d + 1) * 64, c * 128:(c + 1) * 128],
                                                               rhs=WACC[d * 64:(d + 1) * 64, sl], start=True, stop=True), reads=[lw_b, acc_b], writes=[pw_b])
                            dec, dec_b = newt()
                            k.op("act", lambda: nc.scalar.activation(out=dec[:, :n], in_=pw[:, :n], func=AF.Sigmoid,
                                                                    bias=PV[:, O_W0 + d * RT + c:O_W0 + d * RT + c + 1]), reads=[pw_b, PV_b], writes=[dec_b])
                            if CHUNKED:
                                k.op("dve", lambda: nc.vector.tensor_scalar(out=dec[:, :n], in0=dec[:, :n], scalar1=-EM05, scalar2=None, op0=ALU.mult), reads=[dec_b], writes=[dec_b])
                            else:
                                k.op("act", lambda: nc.scalar.activation(out=dec[:, :n], in_=dec[:, :n], func=AF.Exp, scale=-EM05), reads=[dec_b], writes=[dec_b])
                            streams[0 + d] = (dec, dec_b)
                            pa, pa_b = PS.get()
                            k.op("pe", lambda: nc.tensor.matmul(out=pa[:, :n], lhsT=a2t[d * 64:(d + 1) * 64, c * 128:(c + 1) * 128],
                                                               rhs=AACC[d * 64:(d + 1) * 64, sl], start=True, stop=True), reads=[lw_b, acc_b], writes=[pa_b])
                            ic, ic_b = newt()
                            k.op("act", lambda: nc.scalar.activation(out=ic[:, :n], in_=pa[:, :n], func=AF.Sigmoid,
                                                                    bias=PV[:, O_A0 + d * RT + c:O_A0 + d * RT + c + 1]), reads=[pa_b, PV_b], writes=[ic_b])
                            kd, kd_b = newt()
                            k.op("dve", lambda: nc.vector.tensor_scalar(out=kd[:, :n], in0=ic[:, :n], scalar1=PV[:, O_KA + c:O_KA + c + 1],
                                                                        scalar2=omka[:, c:c + 1], op0=ALU.mult, op1=ALU.add), reads=[ic_b, PV_b, omka_b], writes=[kd_b])
                            k.op("pool", lambda: nc.gpsimd.tensor_tensor(out=kd[:, :n], in0=kd[:, :n], in1=KXt[:, sl], op=ALU.mult), reads=[kd_b, KX_b], writes=[kd_b])
                            streams[2 + d] = (kd, kd_b)
                            kds.append((kd, kd_b))
                            bv, bv_b = newt()
                            k.op("pool", lambda: nc.gpsimd.tensor_tensor(out=bv[:, :n], in0=kk[:, :n], in1=ic[:, :n], op=ALU.mult), reads=[kk_b, ic_b], writes=[bv_b])
                            streams[4 + d] = (bv, bv_b)
                        if first:
                            vv, vv_b = newt()
                            k.op("pool", lambda: nc.gpsimd.tensor_copy(out=vv[:, :n], in_=VXt[:, sl]), reads=[VX_b], writes=[vv_b])
                        else:
                            pv_, pv_b = PS.get()
                            k.op("pe", lambda: nc.tensor.matmul(out=pv_[:, :n], lhsT=v2t[:, c * 128:(c + 1) * 128], rhs=VACC[:, sl], start=True, stop=True),
                                 reads=[lw_b, acc_b], writes=[pv_b])
                            sgv, sgv_b = newt()
                            k.op("act", lambda: nc.scalar.activation(out=sgv[:, :n], in_=pv_[:, :n], func=AF.Sigmoid, bias=PV[:, O_V0 + c:O_V0 + c + 1]),
                                 reads=[pv_b, PV_b], writes=[sgv_b])
                            vv, vv_b = newt()
                            k.dma("sp", vv[:, :n], VF[b, c * 128:(c + 1) * 128, sl], reads=[VF_b], writes=[vv_b])
                            k.op("pool", lambda: nc.gpsimd.tensor_tensor(out=vv[:, :n], in0=vv[:, :n], in1=VXt[:, sl], op=ALU.subtract), reads=[vv_b, VX_b], writes=[vv_b])
                            k.op("pool", lambda: nc.gpsimd.tensor_tensor(out=vv[:, :n], in0=vv[:, :n], in1=sgv[:, :n], op=ALU.mult), reads=[vv_b, sgv_b], writes=[vv_b])
                            k.op("pool", lambda: nc.gpsimd.tensor_tensor(out=vv[:, :n], in0=vv[:, :n], in1=VXt[:, sl], op=ALU.add), reads=[vv_b, VX_b], writes=[vv_b])
                        streams[6] = (vv, vv_b)
                        rr, rr_b = newt()
                        k.op("act", lambda: nc.scalar.copy(out=rr[:, :n], in_=RRt[:, sl]), reads=[RR_b], writes=[rr_b])
                        streams[8] = (rr, rr_b)
                        pg, pg_b = PS.get()
                        for kt in range(2):
                            k.op("pe", lambda: nc.tensor.matmul(out=pg[:, :n], lhsT=g2t[:, kt, c * 128:(c + 1) * 128], rhs=GACC[:, kt, sl],
                                                               start=(kt == 0), stop=(kt == 1)), reads=[lw_b, acc_b], writes=[pg_b])
                        gg, gg_b = newt()
                        k.op("act", lambda: nc.scalar.copy(out=gg[:, :n], in_=pg[:, :n]), reads=[pg_b], writes=[gg_b])
                        k.dma("pool", GB[b, 0, c * 128:(c + 1) * 128, sl], gg[:, :n], reads=[gg_b], writes=[GB_b])
                        bs, bs_b = newt()
                        k.op("pool", lambda: nc.gpsimd.tensor_tensor(out=bs[:, :n], in0=kds[0][0][:, :n], in1=kds[1][0][:, :n], op=ALU.add),
                             reads=[kds[0][1], kds[1][1]], writes=[bs_b])
                        k.op("dve", lambda: nc.vector.scalar_tensor_tensor(out=bs[:, :n], in0=bs[:, :n], scalar=PV[:, O_RK + c:O_RK + c + 1], in1=RRt[:, sl],
                                                                           op0=ALU.mult, op1=ALU.mult), reads=[bs_b, PV_b, RR_b], writes=[bs_b])
                        pb, pb_b = PS.get()
                        k.op("pe", lambda: nc.tensor.matmul(out=pb[:, :n], lhsT=bd64[:], rhs=bs[:, :n], start=True, stop=True),
                             reads=[bd64_b, bs_b], writes=[pb_b])
                        bo, bo_b = newt()
                        k.op("dve", lambda: nc.vector.tensor_tensor(out=bo[:, :n], in0=pb[:, :n], in1=vv[:, :n], op=ALU.mult), reads=[pb_b, vv_b], writes=[bo_b])
                        k.dma("pool", GB[b, 1, c * 128:(c + 1) * 128, sl], bo[:, :n], reads=[bo_b], writes=[GB_b])
                        for s_i in range(9):
                            st_, st_b = streams[s_i]
                            if CHUNKED and s_i != 6:
                                k.dma("pool", FM[b, s_i, c * 128:(c + 1) * 128, sl], st_[:, :n], reads=[st_b], writes=[FM_b])
                                continue
                            pt_, pt_b = PS.get()
                            nb_ = n // 128
                            for j in range(nb_):
                                k.op("pe", lambda: nc.tensor.transpose(out=pt_[:, j * 128:(j + 1) * 128], in_=st_[:, j * 128:(j + 1) * 128], identity=ident[:]),
                                     reads=[st_b, ident_b], writes=[pt_b])
                            tko, tko_b = tk.get()
                            evac(s_i, tko[:, :nb_, :].rearrange("p a b -> p (a b)"), pt_[:, :n], [pt_b], [tko_b])
                            k.dma("pool", STR[b, s_i, t0:t0 + n, c * 128:(c + 1) * 128].rearrange("(j p) ch -> p j ch", p=128), tko[:, :nb_, :],
                                  reads=[tko_b], writes=[STR_b])
                k.barrier()
                e2.close()
            for b in range(NB):
                pass1(b)
                pass2(b)
            k.barrier()

    def rwkv_scan(layer):
        with ExitStack() as es:
            def talloc(name, shape, dt=F32):
                return es.enter_context(nc.sbuf_tensor(_u(name), list(shape), dt))
            NP = NB * 2 * NH
            TB = 16
            S = talloc("scS", [NP, 64, 64]); S_b = Buf()
            TM = talloc("scTM", [NP, 64, 64]); TM_b = Buf()
            TM2 = TPool(talloc, "scTM2", [NP, 64, 64], F32, 2)
            SA = talloc("scSA", [NP, 64]); SA_b = Buf()
            inb = TPool(talloc, "scin", [NP, 6, TB, 64], F32, 2)
            yb = TPool(talloc, "scy", [NP, TB, 64], F32, 2)
            k.op("dve", lambda: nc.vector.memset(S[:], 0.0), writes=[S_b])
            sidx = lambda d: (0 + d, 2 + d, 6, 7, 4 + d, 8)
            nblk = TT // TB
            assert TC % TB == 0

            def tokrange(d, n0):
                if d == 0:
                    return slice(n0, n0 + TB, 1)
                if n0 < TC:
                    hi = TC - 1 - n0
                else:
                    hi = TT - 1 - (n0 - TC)
                lo = hi - TB
                return slice(hi, lo if lo >= 0 else None, -1)
            for blk in range(nblk):
                n0 = blk * TB
                it, it_b = inb.get()
                for b in range(NB):
                    for d in range(2):
                        p0 = (b * 2 + d) * NH
                        sl = tokrange(d, n0)
                        for s6, si in enumerate(sidx(d)):
                            k.dma("sp", it[p0:p0 + NH, s6, :, :], STR[b, si, sl, :].rearrange("t (h c) -> h t c", c=64),
                                  reads=[STR_b], writes=[it_b])
                yt, yt_b = yb.get()
                for t in range(TB):
                    w_ = it[:, 0, t, :][:, None, :].to_broadcast([NP, 64, 64])
                    kd_ = it[:, 1, t, :][:, None, :].to_broadcast([NP, 64, 64])
                    v_ = it[:, 2, t, :][:, :, None].to_broadcast([NP, 64, 64])
                    kk_ = it[:, 3, t, :][:, None, :].to_broadcast([NP, 64, 64])
                    b_ = it[:, 4, t, :][:, None, :].to_broadcast([NP, 64, 64])
                    r_ = it[:, 5, t, :][:, None, :].to_broadcast([NP, 64, 64])
                    t2, t2_b = TM2.get()
                    k.op("pool", lambda: nc.gpsimd.tensor_tensor(out=t2[:], in0=v_, in1=kd_, op=ALU.mult), reads=[it_b], writes=[t2_b])
                    k.op("dve", lambda: nc.vector.tensor_tensor(out=TM[:], in0=S[:], in1=kk_, op=ALU.mult), reads=[S_b, it_b], writes=[TM_b])
                    k.op("dve", lambda: nc.vector.tensor_reduce(out=SA[:], in_=TM[:], axis=AX.X, op=ALU.add), reads=[TM_b], writes=[SA_b])
                    k.op("dve", lambda: nc.vector.tensor_tensor(out=S[:], in0=S[:], in1=w_, op=ALU.mult), reads=[S_b, it_b], writes=[S_b])
                    k.op("dve", lambda: nc.vector.tensor_tensor(out=TM[:], in0=SA[:][:, :, None].to_broadcast([NP, 64, 64]), in1=b_, op=ALU.mult),
                         reads=[SA_b, it_b], writes=[TM_b])
                    k.op("dve", lambda: nc.vector.tensor_tensor(out=S[:], in0=S[:], in1=TM[:], op=ALU.subtract), reads=[S_b, TM_b], writes=[S_b])
                    k.op("dve", lambda: nc.vector.tensor_tensor(out=S[:], in0=S[:], in1=t2[:], op=ALU.add), reads=[S_b, t2_b], writes=[S_b])
                    k.op("dve", lambda: nc.vector.tensor_tensor(out=TM[:], in0=S[:], in1=r_, op=ALU.mult), reads=[S_b, it_b], writes=[TM_b])
                    k.op("dve", lambda: nc.vector.tensor_reduce(out=yt[:, t, :], in_=TM[:], axis=AX.X, op=ALU.add), reads=[TM_b], writes=[yt_b])
                for b in range(NB):
                    for d in range(2):
                        p0 = (b * 2 + d) * NH
                        sl = tokrange(d, n0)
                        k.dma("pool", YD[b, d, sl, :].rearrange("t (h c) -> h t c", c=64), yt[p0:p0 + NH, :, :], reads=[yt_b], writes=[YD_b])
            k.barrier()

    def rwkv_scan_chunked(layer):
        with ExitStack() as es:
            def talloc(name, shape, dt=F32):
                return es.enter_context(nc.sbuf_tensor(_u(name), list(shape), dt))
            NCH = TT // 128
            mu2 = talloc("cmu2", [128, 256]); ml2 = talloc("cml2", [128, 256]); msk_b = Buf()
            k.dma("sp", mu2[:], I["k_mu2"], writes=[msk_b])
            k.dma("sp", ml2[:], I["k_ml2"], writes=[msk_b])
            rst = talloc("crst", [128, TT]); rst_b = Buf()
            k.op("pool", lambda: nc.gpsimd.memset(rst[:], 1.0), writes=[rst_b])
            k.op("pool", lambda: nc.gpsimd.memset(rst[:].rearrange("p (c t) -> p c t", t=128)[:, :, 0:1], 0.0), writes=[rst_b])
            raw = TPool(talloc, "craw", [128, TT], F32, 2)
            names = ["LW", "KK", "BV", "KD", "R", "LG", "E"]
            TL = {nm: (talloc("c" + nm, [128, TT]), Buf()) for nm in names}
            GC = talloc("cGC", [128, NCH]); GC_b = Buf()
            BHt = talloc("cBHt", [128, NCH, 128]); KHt = talloc("cKHt", [128, NCH, 128]); VT = talloc("cVT", [128, NCH, 128])
            BHt_b, KHt_b, VT_b = Buf(), Buf(), Buf()
            H = talloc("cH", [128, 64]); H_b = Buf()
            MK_t = TPool(talloc, "cMK", [128, 2, 256], F32, 2)
            MB_t = TPool(talloc, "cMB", [128, 2, 256], F32, 2)
            PK_t = TPool(talloc, "cPK", [128, 6, 256], F32, 2)
            QK_t = TPool(talloc, "cQK", [128, 256], F32, 3)
            W_t = TPool(talloc, "cW", [128, 128], F32, 3)
            YT_t = TPool(talloc, "cYT", [128, 128], F32, 4)

            antiI = talloc("canti", [128, 128])
            k.dma("sp", antiI[:], I["k_anti"], writes=[msk_b])

            def vasc(d, m):
                if d == 0:
                    return slice(m * 128, (m + 1) * 128, 1)
                v0 = m * 128
                hi = (TC - 1 - v0) if v0 < TC else (TT - 1 - (v0 - TC))
                return slice(hi - 127, hi + 1, 1)
            for b in range(NB):
                for d in range(2):
                    for c in range(RT):
                        srcs = {"LW": 0 + d, "KK": 7, "BV": 4 + d, "KD": 2 + d, "R": 8}
                        for nm, si in srcs.items():
                            dst, dst_b = TL[nm]
                            if d == 0:
                                k.dma("sp", dst[:], FM[b, si, c * 128:(c + 1) * 128, :], reads=[FM_b], writes=[dst_b])
                            else:
                                rw_, rw_b = raw.get()
                                k.dma("sp", rw_[:], FM[b, si, c * 128:(c + 1) * 128, :], reads=[FM_b], writes=[rw_b])
                                k.op("pool", lambda: nc.gpsimd.tensor_copy(out=dst[:, 0:TC], in_=rw_[:, TC - 1::-1]), reads=[rw_b], writes=[dst_b])
                                k.op("pool", lambda: nc.gpsimd.tensor_copy(out=dst[:, TC:TT], in_=rw_[:, TT - 1:TC - 1:-1]), reads=[rw_b], writes=[dst_b])
                        for m in range(NCH):
                            if d == 0:
                                k.dma("sp", VT[:, m, :], STR[b, 6, vasc(d, m), c * 128:(c + 1) * 128], reads=[STR_b], writes=[VT_b])
                            else:
                                vtmp, vtmp_b = YT_t.get()
                                k.dma("sp", vtmp[:], STR[b, 6, vasc(d, m), c * 128:(c + 1) * 128], reads=[STR_b], writes=[vtmp_b])
                                pv_, pv_b = PS.get()
                                k.op("pe", lambda: nc.tensor.matmul(out=pv_[:, 0:128], lhsT=antiI[:], rhs=vtmp[:], start=True, stop=True), reads=[msk_b, vtmp_b], writes=[pv_b])
                                evac(m, VT[:, m, :], pv_[:, 0:128], [pv_b], [VT_b])
                        LW, LW_b = TL["LW"]; KK, KK_b = TL["KK"]; BV, BV_b = TL["BV"]; KD, KD_b = TL["KD"]; Rr, R_b = TL["R"]
                        LG, LG_b = TL["LG"]; E, E_b = TL["E"]
                        k.op("dve", lambda: nc.vector.tensor_tensor_scan(out=LG[:], data0=rst[:], data1=LW[:], initial=0.0, op0=ALU.mult, op1=ALU.add),
                             reads=[rst_b, LW_b], writes=[LG_b])
                        k.op("pool", lambda: nc.gpsimd.tensor_tensor(out=E[:], in0=LG[:], in1=LW[:], op=ALU.subtract), reads=[LG_b, LW_b], writes=[E_b])
                        k.op("act", lambda: nc.scalar.activation(out=E[:], in_=E[:], func=AF.Exp), reads=[E_b], writes=[E_b])
                        k.op("dve", lambda: nc.vector.scalar_tensor_tensor(out=KK[:], in0=KK[:], scalar=-1.0, in1=E[:], op0=ALU.mult, op1=ALU.mult),
                             reads=[KK_b, E_b], writes=[KK_b])
                        k.op("act", lambda: nc.scalar.activation(out=E[:], in_=LG[:], func=AF.Exp, scale=-1.0), reads=[LG_b, KK_b], writes=[E_b])
                        k.op("pool", lambda: nc.gpsimd.tensor_tensor(out=BV[:], in0=BV[:], in1=E[:], op=ALU.mult), reads=[BV_b, E_b], writes=[BV_b])
                        k.op("dve", lambda: nc.vector.tensor_tensor(out=KD[:], in0=KD[:], in1=E[:], op=ALU.mult), reads=[KD_b, E_b], writes=[KD_b])
                        k.op("act", lambda: nc.scalar.activation(out=E[:], in_=LG[:], func=AF.Exp), reads=[LG_b, BV_b, KD_b], writes=[E_b])
                        k.op("dve", lambda: nc.vector.tensor_tensor(out=Rr[:], in0=Rr[:], in1=E[:], op=ALU.mult), reads=[R_b, E_b], writes=[R_b])
                        k.op("dve", lambda: nc.vector.tensor_copy(out=GC[:], in_=E[:].rearrange("p (c t) -> p c t", t=128)[:, :, 127]), reads=[E_b], writes=[GC_b])
                        gcb = GC[:][:, :, None].to_broadcast([128, NCH, 128])
                        k.op("dve", lambda: nc.vector.tensor_tensor(out=LG[:].rearrange("p (c t) -> p c t", t=128), in0=BV[:].rearrange("p (c t) -> p c t", t=128),
                                                                    in1=gcb, op=ALU.mult), reads=[BV_b, GC_b, E_b], writes=[LG_b])
                        k.op("pool", lambda: nc.gpsimd.tensor_tensor(out=E[:].rearrange("p (c t) -> p c t", t=128), in0=KD[:].rearrange("p (c t) -> p c t", t=128),
                                                                     in1=gcb, op=ALU.mult), reads=[KD_b, GC_b, R_b], writes=[E_b])
                        for (srcT, srcT_b, dstT, dstT_b) in ((LG, LG_b, BHt, BHt_b), (E, E_b, KHt, KHt_b)):
                            for m0 in range(0, NCH, 4):
                                mm_ = min(4, NCH - m0)
                                pt_, pt_b = PS.get()
                                for j in range(mm_):
                                    k.op("pe", lambda: nc.tensor.transpose(out=pt_[:, j * 128:(j + 1) * 128], in_=srcT[:, (m0 + j) * 128:(m0 + j + 1) * 128], identity=ident[:]),
                                         reads=[srcT_b, ident_b], writes=[pt_b])
                                evac(m0 // 4, dstT[:, m0:m0 + mm_, :].rearrange("p a b -> p (a b)"), pt_[:, :mm_ * 128], [pt_b], [dstT_b])
                        k.op("dve", lambda: nc.vector.memset(H[:], 0.0), writes=[H_b])
                        AT, AT_b, BT, BT_b, KT, KT_b, RTt, RT_b = KK, KK_b, BV, BV_b, KD, KD_b, Rr, R_b
                        def bulk_gen(m):
                            cs = slice(m * 128, (m + 1) * 128)
                            MK, MK_b = MK_t.get(); MB, MB_b = MB_t.get(); PK, PK_b = PK_t.get()
                            ctx_[m] = (MK, MK_b, MB, MB_b, PK, PK_b)
                            for h in range(2):
                                pb = slice(64 * h, 64 * h + 64)
                                pA, pA_b = PS.get()
                                k.op("pe", lambda: nc.tensor.matmul(out=pA[:, 0:128], lhsT=KT[pb, cs], rhs=AT[pb, cs], start=True, stop=True), reads=[KT_b, AT_b], writes=[pA_b])
                                k.op("pe", lambda: nc.tensor.matmul(out=pA[:, 128:256], lhsT=KT[pb, cs], rhs=RTt[pb, cs], start=True, stop=True, skip_group_check=True),
                                     reads=[KT_b, RT_b], writes=[pA_b])
                                k.op("dve", lambda: nc.vector.tensor_tensor(out=MK[:, h, :], in0=pA[:, 0:256], in1=mu2[:], op=ALU.mult), reads=[pA_b, msk_b], writes=[MK_b])
                                pB, pB_b = PS.get()
                                k.op("pe", lambda: nc.tensor.matmul(out=pB[:, 0:128], lhsT=BT[pb, cs], rhs=AT[pb, cs], start=True, stop=True), reads=[BT_b, AT_b], writes=[pB_b])
                                k.op("pe", lambda: nc.tensor.matmul(out=pB[:, 128:256], lhsT=BT[pb, cs], rhs=RTt[pb, cs], start=True, stop=True, skip_group_check=True),
                                     reads=[BT_b, RT_b], writes=[pB_b])
                                k.op("dve", lambda: nc.vector.tensor_tensor(out=MB[:, h, :], in0=pB[:, 0:256], in1=mu2[:], op=ALU.mult), reads=[pB_b, msk_b], writes=[MB_b])
                            pN, pN_b = PS.get()
                            for h in range(2):
                                pb = slice(64 * h, 64 * h + 64)
                                k.op("pe", lambda: nc.tensor.matmul(out=pN[:, 128 * h:128 * h + 128], lhsT=AT[pb, cs], rhs=BT[pb, cs], start=True, stop=True, skip_group_check=True),
                                     reads=[AT_b, BT_b], writes=[pN_b])
                            Q, Q_b = QK_t.get()
                            k.op("dve", lambda: nc.vector.tensor_tensor(out=Q[:], in0=pN[:, 0:256], in1=ml2[:], op=ALU.mult), reads=[pN_b, msk_b], writes=[Q_b])

                            def Pk(kk_, h):
                                return MB[:, h, 0:128] if kk_ == 0 else PK[:, kk_ - 1, 128 * h:128 * h + 128]
                            for kk_ in range(6):
                                yield
                                pP, pP_b = PS.get()
                                pQ, pQ_b = PS.get()
                                for h in range(2):
                                    k.op("pe", lambda: nc.tensor.matmul(out=pP[:, 128 * h:128 * h + 128], lhsT=Q[:, 128 * h:128 * h + 128], rhs=Pk(kk_, h),
                                                                       start=True, stop=True, skip_group_check=True), reads=[Q_b, MB_b, PK_b], writes=[pP_b])
                                    if kk_ < 5:
                                        k.op("pe", lambda: nc.tensor.matmul(out=pQ[:, 128 * h:128 * h + 128], lhsT=Pk(kk_, h), rhs=Q[:, 128 * h:128 * h + 128],
                                                                           start=True, stop=True, skip_group_check=True), reads=[Q_b, MB_b, PK_b], writes=[pQ_b])
                                k.op("act", lambda: nc.scalar.copy(out=PK[:, kk_, :], in_=pP[:, 0:256]), reads=[pP_b], writes=[PK_b])
                                if kk_ < 5:
                                    Q2, Q2_b = QK_t.get()
                                    k.op("dve", lambda: nc.vector.tensor_copy(out=Q2[:], in_=pQ[:, 0:256]), reads=[pQ_b], writes=[Q2_b])
                                    Q, Q_b = Q2, Q2_b
                            yield

                        def chain_gen(m):
                            cs = slice(m * 128, (m + 1) * 128)
                            MK, MK_b, MB, MB_b, PK, PK_b = ctx_[m]

                            def Pk(kk_, h):
                                return MB[:, h, 0:128] if kk_ == 0 else PK[:, kk_ - 1, 128 * h:128 * h + 128]
                            pW, pW_b = PS.get()
                            for h in range(2):
                                pb = slice(64 * h, 64 * h + 64)
                                hc = slice(64 * h, 64 * h + 64)
                                k.op("pe", lambda: nc.tensor.matmul(out=pW[:, hc], lhsT=AT[pb, cs], rhs=H[pb, :], start=True, stop=False, skip_group_check=True),
                                     reads=[AT_b, H_b], writes=[pW_b])
                                k.op("pe", lambda: nc.tensor.matmul(out=pW[:, hc], lhsT=MK[:, h, 0:128], rhs=VT[:, m, hc], start=False, stop=True, skip_group_check=True),
                                     reads=[MK_b, VT_b], writes=[pW_b])
                            Wt, Wt_b = W_t.get()
                            k.op("act", lambda: nc.scalar.copy(out=Wt[:], in_=pW[:, 0:128]), reads=[pW_b], writes=[Wt_b])
                            for kk_ in range(7):
                                yield
                                pH, pH_b = PS.get()
                                for h in range(2):
                                    hc = slice(64 * h, 64 * h + 64)
                                    k.op("pe", lambda: nc.tensor.matmul(out=pH[:, hc], lhsT=Pk(kk_, h), rhs=Wt[:, hc], start=True, stop=True, skip_group_check=True),
                                         reads=[MB_b, PK_b, Wt_b], writes=[pH_b])
                                W2, W2_b = W_t.get()
                                k.op("dve", lambda: nc.vector.tensor_tensor(out=W2[:], in0=pH[:, 0:128], in1=Wt[:], op=ALU.add), reads=[pH_b, Wt_b], writes=[W2_b])
                                Wt, Wt_b = W2, W2_b
                            U, U_b = Wt, Wt_b
                            pY, pY_b = PS.get()
                            for h in range(2):
                                pb = slice(64 * h, 64 * h + 64)
                                hc = slice(64 * h, 64 * h + 64)
                                k.op("pe", lambda: nc.tensor.matmul(out=pY[:, hc], lhsT=RTt[pb, cs], rhs=H[pb, :], start=True, stop=False, skip_group_check=True),
                                     reads=[RT_b, H_b], writes=[pY_b])
                                k.op("pe", lambda: nc.tensor.matmul(out=pY[:, hc], lhsT=MB[:, h, 128:256], rhs=U[:, hc], start=False, stop=False, skip_group_check=True),
                                     reads=[MB_b, U_b], writes=[pY_b])
                                k.op("pe", lambda: nc.tensor.matmul(out=pY[:, hc], lhsT=MK[:, h, 128:256], rhs=VT[:, m, hc], start=False, stop=True, skip_group_check=True),
                                     reads=[MK_b, VT_b], writes=[pY_b])
                            YT, YT_b = YT_t.get()
                            k.op("act", lambda: nc.scalar.copy(out=YT[:], in_=pY[:, 0:128]), reads=[pY_b], writes=[YT_b])
                            if d == 1:
                                pr_, pr_b = PS.get()
                                k.op("pe", lambda: nc.tensor.matmul(out=pr_[:, 0:128], lhsT=antiI[:], rhs=YT[:], start=True, stop=True), reads=[msk_b, YT_b], writes=[pr_b])
                                YT, YT_b = YT_t.get()
                                k.op("dve", lambda: nc.vector.tensor_copy(out=YT[:], in_=pr_[:, 0:128]), reads=[pr_b], writes=[YT_b])
                            k.dma("pool", YD[b, d, vasc(d, m), c * 128:(c + 1) * 128], YT[:], reads=[YT_b], writes=[YD_b])
                            pS, pS_b = PS.get()
                            for h in range(2):
                                hc = slice(64 * h, 64 * h + 64)
                                k.op("pe", lambda: nc.tensor.matmul(out=pS[:, hc], lhsT=BHt[:, m, :], rhs=U[:, hc], start=True, stop=False, skip_group_check=True),
                                     reads=[BHt_b, U_b], writes=[pS_b])
                                k.op("pe", lambda: nc.tensor.matmul(out=pS[:, hc], lhsT=KHt[:, m, :], rhs=VT[:, m, hc], start=False, stop=True, skip_group_check=True),
                                     reads=[KHt_b, VT_b], writes=[pS_b])
                            for h in range(2):
                                pb = slice(64 * h, 64 * h + 64)
                                hc = slice(64 * h, 64 * h + 64)
                                k.op("dve", lambda: nc.vector.scalar_tensor_tensor(out=H[pb, :], in0=H[pb, :], scalar=GC[pb, m:m + 1], in1=pS[pb, hc],
                                                                                   op0=ALU.mult, op1=ALU.add), reads=[H_b, GC_b, pS_b], writes=[H_b])
                            yield
                        ctx_ = {}
                        for _ in bulk_gen(0):
                            pass
                        for m in range(NCH):
                            gens = [chain_gen(m)]
                            if m + 1 < NCH:
                                gens.append(bulk_gen(m + 1))
                            while gens:
                                for g_ in list(gens):
                                    try:
                                        next(g_)
                                    except StopIteration:
                                        gens.remove(g_)
            k.barrier()

    def rwkv_out(layer):
        with ExitStack() as es:
            def talloc(name, shape, dt=F32):
                return es.enter_context(nc.sbuf_tensor(_u(name), list(shape), dt))
            y0_t = TPool(talloc, "roy0", [128, RW], F32, 2)
            y1_t = TPool(talloc, "roy1", [128, RW], F32, 2)
            sq_t = TPool(talloc, "rosq", [128, RW], F32, 1)
            st_t = TPool(talloc, "rost", [128, 3, NH], F32, 2)
            gb_t = TPool(talloc, "rogb", [128, 2, RT, 128], F32, 2)
            o_t = TPool(talloc, "roo", [128, RT, 128], F32, 2)
            geps = talloc("rogeps", [128, 1]); geps_b = Buf()
            k.op("dve", lambda: nc.vector.memset(geps[:], GN_EPS), writes=[geps_b])
            for b in range(NB):
                for tt in range(TT // 128):
                    ts_ = slice(tt * 128, (tt + 1) * 128)
                    y0, y0_b = y0_t.get(); y1, y1_b = y1_t.get()
                    k.dma("sp", y0[:], YD[b, 0, ts_, :], reads=[YD_b], writes=[y0_b])
                    k.dma("sp", y1[:], YD[b, 1, ts_, :], reads=[YD_b], writes=[y1_b])
                    gbt, gbt_b = gb_t.get()
                    for s_ in range(2):
                        k.dma("sp", gbt[:, s_, :, :], GB[b, s_].rearrange("(c p) t -> p c t", p=128)[:, :, ts_], reads=[GB_b], writes=[gbt_b])
                    st, st_b = st_t.get()
                    sq, sq_b = sq_t.get()
                    y3 = y0[:].rearrange("p (h c) -> p h c", c=64)
                    s3 = sq[:].rearrange("p (h c) -> p h c", c=64)
                    k.op("dve", lambda: nc.vector.tensor_tensor(out=y0[:], in0=y0[:], in1=y1[:], op=ALU.add), reads=[y0_b, y1_b], writes=[y0_b])
                    k.op("dve", lambda: nc.vector.tensor_reduce(out=st[:, 0, :], in_=y3, axis=AX.X, op=ALU.add), reads=[y0_b], writes=[st_b])
                    k.op("dve", lambda: nc.vector.tensor_scalar(out=st[:, 0, :], in0=st[:, 0, :], scalar1=1.0 / 64, scalar2=None, op0=ALU.mult), reads=[st_b], writes=[st_b])
                    k.op("dve", lambda: nc.vector.tensor_tensor(out=y3, in0=y3, in1=st[:, 0, :][:, :, None].to_broadcast([128, NH, 64]), op=ALU.subtract),
                         reads=[y0_b, st_b], writes=[y0_b])
                    k.op("act", lambda: nc.scalar.activation(out=sq[:], in_=y0[:], func=AF.Square), reads=[y0_b], writes=[sq_b])
                    k.op("dve", lambda: nc.vector.tensor_reduce(out=st[:, 1, :], in_=s3, axis=AX.X, op=ALU.add), reads=[sq_b], writes=[st_b])
                    k.op("act", lambda: nc.scalar.activation(out=st[:, 2, :], in_=st[:, 1, :], func=AF.Sqrt, scale=1.0 / 64, bias=geps[:, 0:1]),
                         reads=[st_b, geps_b], writes=[st_b])
                    k.op("dve", lambda: nc.vector.reciprocal(out=st[:, 2, :], in_=st[:, 2, :]), reads=[st_b], writes=[st_b])
                    k.op("dve", lambda: nc.vector.tensor_tensor(out=y3, in0=y3, in1=st[:, 2, :][:, :, None].to_broadcast([128, NH, 64]), op=ALU.mult),
                         reads=[y0_b, st_b], writes=[y0_b])
                    ot, ot_b = o_t.get()
                    for q in range(3):
                        ps, ps_b = PS.get()
                        for j in range(4):
                            c = q * 4 + j
                            k.op("pe", lambda: nc.tensor.transpose(out=ps[:, j * 128:(j + 1) * 128], in_=y0[:, c * 128:(c + 1) * 128], identity=ident[:]),
                                 reads=[y0_b, ident_b], writes=[ps_b])
                        for j in range(4):
                            c = q * 4 + j
                            k.op("act", lambda: nc.scalar.activation(out=ot[:, c, :], in_=ps[:, j * 128:(j + 1) * 128], func=AF.Identity,
                                                                    scale=PV[:, O_LNW + c:O_LNW + c + 1], bias=PV[:, O_LNB + c:O_LNB + c + 1]),
                                 reads=[ps_b, PV_b], writes=[ot_b])
                    k.op("dve", lambda: nc.vector.tensor_tensor(out=ot[:], in0=ot[:], in1=gbt[:, 1, :, :], op=ALU.add), reads=[ot_b, gbt_b], writes=[ot_b])
                    k.op("pool", lambda: nc.gpsimd.tensor_tensor(out=ot[:], in0=ot[:], in1=gbt[:, 0, :, :], op=ALU.mult), reads=[ot_b, gbt_b], writes=[ot_b])
                    k.dma("pool", MIX[b, S5W:D, :].rearrange("(c p) t -> p c t", p=128)[:, :, ts_], ot[:], reads=[ot_b], writes=[MIX_b])
            k.barrier()

    for layer in range(DEPTH):
        with ExitStack() as es:
            load_params(layer, es)
            compute_mod(layer, es)
            k.barrier()
        ffn_phase(layer, 0, 0)
        if stop_after == "ffn1":
            break
        win_phase(layer)
        if stop_after == "win":
            break
        s5_phase(layer)
        if stop_after == "s5":
            break
        rwkv_prep(layer)
        if stop_after == "prep":
            break
        if CHUNKED:
            rwkv_scan_chunked(layer)
        else:
            rwkv_scan(layer)
        if stop_after == "scan":
            break
        rwkv_out(layer)
        wout_phase(layer)
        if stop_after == "wout":
            break
        ffn_phase(layer, 1, 2)

    with ExitStack() as es:
        def talloc(name, shape, dt=F32):
            return es.enter_context(nc.sbuf_tensor(_u(name), list(shape), dt))
        xs_t = TPool(talloc, "fxs", [128, DTL, 512], F32, 1)
        tmp_t = TPool(talloc, "ftmp", [128, 512], F32, 3)
        rs_t = TPool(talloc, "frs", [128, 512], F32, 1)
        ot_t = TPool(talloc, "fot", [128, D], F32, 2)
        for b in range(NB):
            for (t0, n, is_ctx) in chunks_of(TC, T):
                if is_ctx:
                    continue
                xs, xs_b = xs_t.get()
                k.dma("sp", xs[:, :, :n], XT[b].rearrange("(dt p) t -> p dt t", p=128)[:, :, t0:t0 + n], reads=[XT_b], writes=[xs_b])
                if final_norm:
                    ps, ps_b = PS.get()
                    for dt in range(DTL):
                        sq, sq_b = tmp_t.get()
                        k.op("act", lambda dt=dt, sq=sq: nc.scalar.activation(out=sq[:, :n], in_=xs[:, dt, :n], func=AF.Square),
                             reads=[xs_b], writes=[sq_b])
                        k.op("pe", lambda dt=dt, sq=sq: nc.tensor.matmul(out=ps[:, :n], lhsT=ones[:], rhs=sq[:, :n],
                                                                       start=(dt == 0), stop=(dt == DTL - 1)),
                             reads=[sq_b, ones_b], writes=[ps_b])
                    rs, rs_b = rs_t.get()
                    k.op("act", lambda: nc.scalar.activation(out=rs[:, :n], in_=ps[:, :n], func=AF.Sqrt, scale=1.0 / D, bias=epsT[:, 0:1]),
                         reads=[ps_b, eps_b], writes=[rs_b])
                    k.op("dve", lambda: nc.vector.reciprocal(out=rs[:, :n], in_=rs[:, :n]), reads=[rs_b], writes=[rs_b])
                    for dt in range(DTL):
                        k.op("dve", lambda dt=dt: nc.vector.scalar_tensor_tensor(
                            out=xs[:, dt, :n], in0=xs[:, dt, :n], scalar=FG[:, dt:dt + 1], in1=rs[:, :n],
                            op0=ALU.mult, op1=ALU.mult), reads=[xs_b, FG_b, rs_b], writes=[xs_b])
                for tq in range(n // 128):
                    ot, ot_b = ot_t.get()
                    for q in range(4):
                        ps, ps_b = PS.get()
                        for j in range(4):
                            dt = q * 4 + j
                            k.op("pe", lambda dt=dt, j=j, ps=ps: nc.tensor.transpose(
                                out=ps[:, j * 128:(j + 1) * 128], in_=xs[:, dt, tq * 128:(tq + 1) * 128], identity=ident[:]),
                                reads=[xs_b, ident_b], writes=[ps_b])
                        if q % 2 == 0:
                            k.op("dve", lambda q=q, ps=ps, ot=ot: nc.vector.tensor_copy(out=ot[:, q * 512:(q + 1) * 512], in_=ps[:]),
                                 reads=[ps_b], writes=[ot_b])
                        else:
                            k.op("act", lambda q=q, ps=ps, ot=ot: nc.scalar.copy(out=ot[:, q * 512:(q + 1) * 512], in_=ps[:]),
                                 reads=[ps_b], writes=[ot_b])
                    tl = t0 - TC + tq * 128
                    k.dma("pool", out_ap[b, tl:tl + 128, :], ot[:], reads=[ot_b], writes=[OUT_b])
        k.barrier()
    print("build: ninst", k.ninst)
    return nc


def consts():
    ident = np.eye(128, dtype=np.float32)
    ones = np.ones((128, 128), np.float32)
    bd = np.zeros((128, 128), np.float32)
    bd[:64, :64] = 1.0
    bd[64:, 64:] = 1.0
    r = np.arange(128)
    su = (r[None, :] > r[:, None]).astype(np.float32)
    iu = (r[None, :] >= r[:, None]).astype(np.float32)
    sl = (r[None, :] < r[:, None]).astype(np.float32)
    return {"k_ident": ident, "k_ones": ones, "k_bd64": bd,
            "k_mu2": np.concatenate([su, iu], axis=1), "k_ml2": np.concatenate([sl, sl], axis=1),
            "k_anti": np.ascontiguousarray(np.eye(128, dtype=np.float32)[::-1])}


def run(cfg, inputs, n_cores, **bkw):
    NB = cfg["NB"]
    nc = build(cfg, **bkw)
    cs = consts()

    def tile_w(w):
        w = np.asarray(w, dtype=np.float32)
        lead = w.shape[:-2]
        K_, N_ = w.shape[-2:]
        w = w.reshape(lead + (K_ // 128, 128, N_ // 128, 128))
        nl = len(lead)
        perm = tuple(range(nl)) + (nl + 2, nl + 1, nl, nl + 3)
        return np.ascontiguousarray(w.transpose(perm))
    pre = {name: tile_w(inputs[name]) for name in ("ffn_w1", "ffn_w2", "w_in", "w_out")}
    in_maps = []
    for c in range(n_cores):
        m = {}
        for name, v in inputs.items():
            v = pre.get(name, v)
            v = np.ascontiguousarray(v, dtype=np.float32)
            if name in ("x", "c", "ctx"):
                m[name] = v[c * NB:(c + 1) * NB]
            else:
                m[name] = v
        if cfg["DEPTH"] == 1:
            for name in ("rwkv_mu_m", "rwkv_v0", "rwkv_v1", "rwkv_v2"):
                if m[name].shape[0] == 0:
                    m[name] = np.zeros((1,) + m[name].shape[1:], np.float32)
        m.update(cs)
        in_maps.append(m)
    res = run_bass_kernel_spmd(nc, in_maps, core_ids=list(range(n_cores)))
    if getattr(res, "exec_time_ns", None):
        print("exec_time_ns", res.exec_time_ns)
    if bkw.get("dbg"):
        return res.results
    return np.concatenate([r["out"] for r in res.results], axis=0)


def kernel(**inputs):
    return run(FULL_CFG, inputs, 8)
```
